# Optimizing a Trainium2 kernel written in Bass

```python
import jax, jax.numpy as jnp
from jax import lax
import numpy as np

D_MODEL = 1024
BATCH = 8
SEQ = 8192
DEPTH = 1

CHUNK = 64
Q_BLOCK = 128
HEAD_DIM_RWKV = 64
RWKV_DIM = D_MODEL // 2
RWKV_HEADS = RWKV_DIM // HEAD_DIM_RWKV
DECAY_LORA = 64
ICLR_LORA = 64
GATE_LORA = 160
GN_EPS = 64e-5
MLA_HEADS = D_MODEL // 128
Q_LORA = 384
KV_LORA = 256
QK_NOPE = 64
QK_ROPE = 32
V_HEAD = 64
QK_HEAD = QK_NOPE + QK_ROPE
ROPE_THETA = 10000.0
D_FF = 2752
LN_EPS = 1e-5
RMS_EPS = 1e-6
ALPHA = (2.0 * DEPTH) ** 0.25
BETA = (8.0 * DEPTH) ** -0.25
SHIFT_COLS = 3 * RWKV_DIM + DECAY_LORA + ICLR_LORA + GATE_LORA
MLA_COLS = Q_LORA + KV_LORA + QK_ROPE
IN_COLS = SHIFT_COLS + MLA_COLS + 2 * D_MODEL

kernel_name = "hybrid_rwkv7_mla_macaron_deepnorm"


def layer_norm(x, g, b):
    xf = x.astype(jnp.float32)
    mu = jnp.mean(xf, axis=-1, keepdims=True)
    var = jnp.mean(jnp.square(xf - mu), axis=-1, keepdims=True)
    y = (xf - mu) * lax.rsqrt(var + LN_EPS)
    return (y * g.astype(jnp.float32) + b.astype(jnp.float32)).astype(x.dtype)


def rms_norm(x, g):
    xf = x.astype(jnp.float32)
    y = xf * lax.rsqrt(jnp.mean(jnp.square(xf), axis=-1, keepdims=True) + RMS_EPS)
    return (y * g.astype(jnp.float32)).astype(x.dtype)


def swiglu(x, w1, w3, w2):
    return (jax.nn.silu(x @ w1) * (x @ w3)) @ w2


def rope_tables(seq):
    inv_freq = ROPE_THETA ** (-jnp.arange(0, QK_ROPE, 2, dtype=jnp.float32) / QK_ROPE)
    ang = jnp.arange(seq, dtype=jnp.float32)[:, None] * inv_freq[None, :]
    return jnp.cos(ang), jnp.sin(ang)


def apply_rope(x, cos, sin):
    xf = x.astype(jnp.float32)
    x1, x2 = jnp.split(xf, 2, axis=-1)
    return jnp.concatenate([x1 * cos - x2 * sin, x2 * cos + x1 * sin], axis=-1).astype(x.dtype)


def rwkv7_mixer(p, mu_shift, w0, w_decay_up, a0, w_iclr_up, w_gate_up, k_k, k_a, r_k, gn_g, gn_b):
    bsz, seq, _ = p.shape
    h, n = RWKV_HEADS, HEAD_DIM_RWKV
    prev = jnp.pad(p[:, :-1], ((0, 0), (1, 0), (0, 0)))
    p = p + mu_shift * (prev - p)
    r, k, v, wd, ad, gd = jnp.split(
        p, [RWKV_DIM, 2 * RWKV_DIM, 3 * RWKV_DIM, 3 * RWKV_DIM + DECAY_LORA,
            3 * RWKV_DIM + DECAY_LORA + ICLR_LORA], axis=-1)
    w = -jax.nn.softplus(-(w0 + jnp.tanh(wd) @ w_decay_up)) - 0.5
    decay = jnp.exp(-jnp.exp(w.astype(jnp.float32)))
    a = jax.nn.sigmoid(a0 + ad @ w_iclr_up)
    g = jax.nn.sigmoid(gd) @ w_gate_up
    kk = (k * k_k).reshape(bsz, seq, h, n).astype(jnp.float32)
    kk = kk * lax.rsqrt(jnp.maximum(jnp.sum(kk * kk, axis=-1, keepdims=True), 1e-24))
    k = k * (1.0 + (a - 1.0) * k_a)

    def heads(t):
        return t.reshape(bsz, seq, h, n).astype(jnp.float32)

    r_h, k_h, v_h, a_h, w_h = heads(r), heads(k), heads(v), heads(a), heads(decay)

    def step(state, inp):
        r_t, w_t, k_t, v_t, kk_t, b_t = inp
        sa = jnp.einsum('bhvk,bhk->bhv', state, -kk_t)
        state = (state * w_t[:, :, None, :] + sa[..., None] * b_t[:, :, None, :]
                 + v_t[..., None] * k_t[:, :, None, :])
        return state, jnp.einsum('bhvk,bhk->bhv', state, r_t)

    xs = tuple(jnp.moveaxis(t, 1, 0) for t in (r_h, w_h, k_h, v_h, kk, kk * a_h))
    state0 = jnp.zeros((bsz, h, n, n), jnp.float32)
    _, y = lax.scan(step, state0, xs)
    y = jnp.moveaxis(y, 0, 1)
    mu = jnp.mean(y, axis=-1, keepdims=True)
    var = jnp.mean(jnp.square(y - mu), axis=-1, keepdims=True)
    y = ((y - mu) * lax.rsqrt(var + GN_EPS) * gn_g.astype(jnp.float32).reshape(h, n)
         + gn_b.astype(jnp.float32).reshape(h, n))
    bonus = jnp.sum(r_h * k_h * r_k.astype(jnp.float32), axis=-1, keepdims=True) * v_h
    return (y + bonus).reshape(bsz, seq, RWKV_DIM).astype(p.dtype) * g


def mla_mixer(p, q_norm_g, w_q_up, kv_norm_g, w_kv_up):
    bsz, seq, _ = p.shape
    h = MLA_HEADS
    q_lat, kv_lat, k_pe = jnp.split(p, [Q_LORA, Q_LORA + KV_LORA], axis=-1)
    q = (rms_norm(q_lat, q_norm_g) @ w_q_up).reshape(bsz, seq, h, QK_HEAD)
    q_nope, q_pe = jnp.split(q, [QK_NOPE], axis=-1)
    kv = (rms_norm(kv_lat, kv_norm_g) @ w_kv_up).reshape(bsz, seq, h, QK_NOPE + V_HEAD)
    k_nope, v = jnp.split(kv, [QK_NOPE], axis=-1)
    cos, sin = rope_tables(seq)
    q_pe = apply_rope(q_pe, cos[None, :, None, :], sin[None, :, None, :])
    k_pe = apply_rope(k_pe, cos[None], sin[None])
    scale = QK_HEAD ** -0.5
    n_blocks = seq // Q_BLOCK
    qn_b = q_nope.reshape(bsz, n_blocks, Q_BLOCK, h, QK_NOPE).swapaxes(0, 1)
    qp_b = q_pe.reshape(bsz, n_blocks, Q_BLOCK, h, QK_ROPE).swapaxes(0, 1)
    key_chunk = jnp.arange(seq) // CHUNK

    def block(args):
        qn, qp, blk = args
        s = (jnp.einsum('bqhd,bkhd->bhqk', qn, k_nope)
             + jnp.einsum('bqhr,bkr->bhqk', qp, k_pe)).astype(jnp.float32) * scale
        q_chunk = (blk * Q_BLOCK + jnp.arange(Q_BLOCK)) // CHUNK
        mask = key_chunk[None, :] <= q_chunk[:, None]
        s = jnp.where(mask[None, None], s, -jnp.inf)
        prob = jax.nn.softmax(s, axis=-1).astype(v.dtype)
        return jnp.einsum('bhqk,bkhd->bqhd', prob, v)

    o = lax.map(block, (qn_b, qp_b, jnp.arange(n_blocks)))
    return o.swapaxes(0, 1).reshape(bsz, seq, h * V_HEAD)


def setup_inputs(seed: int = 0) -> dict:
    key = jax.random.key(seed)
    ks = jax.random.split(key, 40)
    f32 = jnp.float32

    def nrm(i, shape, scale):
        return scale * jax.random.normal(ks[i], shape, f32)

    L = DEPTH
    w0_base = -6.0 + 5.0 * (jnp.arange(RWKV_DIM, dtype=f32) / (RWKV_DIM - 1)) ** 0.9
    return {
        "x": nrm(0, (BATCH, SEQ, D_MODEL), 1.0),
        "ffn1_w1": nrm(1, (L, D_MODEL, D_FF), D_MODEL ** -0.5),
        "ffn1_w3": nrm(2, (L, D_MODEL, D_FF), D_MODEL ** -0.5),
        "ffn1_w2": nrm(3, (L, D_FF, D_MODEL), BETA * D_FF ** -0.5),
        "ln1_g": 1.0 + nrm(4, (L, D_MODEL), 0.02),
        "ln1_b": nrm(5, (L, D_MODEL), 0.02),
        "w_in": nrm(6, (L, D_MODEL, IN_COLS), D_MODEL ** -0.5),
        "mu_shift": jax.random.uniform(ks[7], (L, SHIFT_COLS), f32),
        "w0": w0_base[None, :] + nrm(8, (L, RWKV_DIM), 0.1),
        "w_decay_up": nrm(9, (L, DECAY_LORA, RWKV_DIM), 0.1 * DECAY_LORA ** -0.5),
        "a0": nrm(10, (L, RWKV_DIM), 0.1),
        "w_iclr_up": nrm(11, (L, ICLR_LORA, RWKV_DIM), ICLR_LORA ** -0.5),
        "w_gate_up": nrm(12, (L, GATE_LORA, RWKV_DIM), GATE_LORA ** -0.5),
        "k_k": 0.85 + nrm(13, (L, RWKV_DIM), 0.05),
        "k_a": 1.0 + nrm(14, (L, RWKV_DIM), 0.05),
        "r_k": nrm(15, (L, RWKV_HEADS, HEAD_DIM_RWKV), 0.1),
        "gn_g": 1.0 + nrm(16, (L, RWKV_DIM), 0.02),
        "gn_b": nrm(17, (L, RWKV_DIM), 0.02),
        "q_norm_g": 1.0 + nrm(18, (L, Q_LORA), 0.02),
        "w_q_up": nrm(19, (L, Q_LORA, MLA_HEADS * QK_HEAD), Q_LORA ** -0.5),
        "kv_norm_g": 1.0 + nrm(20, (L, KV_LORA), 0.02),
        "w_kv_up": nrm(21, (L, KV_LORA, MLA_HEADS * (QK_NOPE + V_HEAD)), KV_LORA ** -0.5),
        "w_up_rwkv": nrm(22, (L, RWKV_DIM, D_MODEL), RWKV_DIM ** -0.5),
        "w_up_mla": nrm(23, (L, MLA_HEADS * V_HEAD, D_MODEL), (MLA_HEADS * V_HEAD) ** -0.5),
        "w_o": nrm(24, (L, D_MODEL, D_MODEL), BETA * D_MODEL ** -0.5),
        "ln2_g": 1.0 + nrm(25, (L, D_MODEL), 0.02),
        "ln2_b": nrm(26, (L, D_MODEL), 0.02),
        "ffn2_w1": nrm(27, (L, D_MODEL, D_FF), D_MODEL ** -0.5),
        "ffn2_w3": nrm(28, (L, D_MODEL, D_FF), D_MODEL ** -0.5),
        "ffn2_w2": nrm(29, (L, D_FF, D_MODEL), BETA * D_FF ** -0.5),
        "ln3_g": 1.0 + nrm(30, (L, D_MODEL), 0.02),
        "ln3_b": nrm(31, (L, D_MODEL), 0.02),
    }


def reference(x, ffn1_w1, ffn1_w3, ffn1_w2, ln1_g, ln1_b,
              w_in, mu_shift, w0, w_decay_up, a0, w_iclr_up, w_gate_up, k_k, k_a, r_k, gn_g, gn_b,
              q_norm_g, w_q_up, kv_norm_g, w_kv_up,
              w_up_rwkv, w_up_mla, w_o, ln2_g, ln2_b,
              ffn2_w1, ffn2_w3, ffn2_w2, ln3_g, ln3_b):
    h = x
    for l in range(DEPTH):
        h = layer_norm(ALPHA * h + 0.5 * swiglu(h, ffn1_w1[l], ffn1_w3[l], ffn1_w2[l]), ln1_g[l], ln1_b[l])
        proj = h @ w_in[l]
        p_shift, p_mla, gate_logits = jnp.split(proj, [SHIFT_COLS, SHIFT_COLS + MLA_COLS], axis=-1)
        y_rwkv = rwkv7_mixer(p_shift, mu_shift[l], w0[l], w_decay_up[l], a0[l], w_iclr_up[l],
                             w_gate_up[l], k_k[l], k_a[l], r_k[l], gn_g[l], gn_b[l]) @ w_up_rwkv[l]
        y_mla = mla_mixer(p_mla, q_norm_g[l], w_q_up[l], kv_norm_g[l], w_kv_up[l]) @ w_up_mla[l]
        g_rwkv, g_mla = jnp.split(jax.nn.sigmoid(gate_logits), 2, axis=-1)
        mix = (g_rwkv * y_rwkv + g_mla * y_mla) @ w_o[l]
        h = layer_norm(ALPHA * h + mix, ln2_g[l], ln2_b[l])
        h = layer_norm(ALPHA * h + 0.5 * swiglu(h, ffn2_w1[l], ffn2_w3[l], ffn2_w2[l]), ln3_g[l], ln3_b[l])
    return h
```

```python
import contextlib
import numpy as np
import ml_dtypes
import concourse.bass as bass
import concourse.mybir as mybir
from concourse.bass_utils import run_bass_kernel_spmd

F32 = mybir.dt.float32
BF16 = mybir.dt.bfloat16
AF = mybir.ActivationFunctionType
ALU = mybir.AluOpType

D = 1024
DFF = 2752
T = 256
NKC = D // 128
NJ = (DFF + 127) // 128
ALPHA = 2.0 ** 0.25
LN_EPS = 1e-5
WCOLS = 2048
TPB = WCOLS // 128
NSLOT = 6
PREFETCH = 4
NPRM = 160
KSEG = 1024
CH = 64
QSCALE = 96.0 ** -0.5
C0 = float(np.exp(-0.5))
GN_EPS = 64e-5
NG = 2
HG = 8 // NG


def inter(gens):
    gens = list(gens)
    while gens:
        for g_ in list(gens):
            try:
                next(g_)
            except StopIteration:
                gens.remove(g_)
        yield


def interleave(gens):
    for _ in inter(gens):
        pass


ENGS = ("pe", "act", "dve", "pool", "sp")
BIG = 1 << 30


class Buf:
    def __init__(self, name, t=None, sem=None):
        self.name = name
        self.t = t
        self.sem = sem
        self.semval = 0
        self.lw = None
        self.rd = {}


class Sched:
    def __init__(self, nc, stack):
        self.nc = nc
        self.stack = stack
        self.streams = {e: [] for e in ENGS}
        self.esem = {}
        for e in ("pe", "act", "dve", "pool"):
            self.esem[e] = stack.enter_context(nc.semaphore("es_" + e))
        self.ecnt = {e: 0 for e in ENGS}
        self.waited = {e: {} for e in ENGS}
        self.nsem = 4
        self.bigval = None

    def new_sem(self, name):
        self.nsem += 1
        return self.stack.enter_context(self.nc.semaphore(name))

    def _deps(self, eng, reads, writes):
        need = {}

        def add(sv):
            if sv is None:
                return
            s, v = sv
            k = id(s)
            if k not in need or need[k][1] < v:
                need[k] = (s, v)
        for b in reads:
            add(b.lw)
        for b in writes:
            add(b.lw)
            for k, sv in b.rd.items():
                add(sv)
        waits = []
        w = self.waited[eng]
        for k, (s, v) in need.items():
            if eng == "pe" and s is self.esem["pe"]:
                continue
            if w.get(k, 0) < v:
                w[k] = v
                waits.append((s, v))
        return waits

    def _mark(self, tag, reads, writes):
        for b in reads:
            k = id(tag[0])
            if k not in b.rd or b.rd[k][1] < tag[1]:
                b.rd[k] = tag
        for b in writes:
            b.lw = tag
            b.rd = {}

    def op(self, eng, fn, reads=(), writes=(), signal=True):
        waits = self._deps(eng, reads, writes)
        s = self.esem[eng]
        idx = self.ecnt[eng] + 1
        if signal:
            self.ecnt[eng] = idx
        self.streams[eng].append((waits, fn, (s, 1) if signal else None))
        self._mark((s, idx), reads, writes)

    def dma(self, queue, fn, reads, writes, sembuf):
        waits = self._deps(queue, reads, writes)
        sembuf.semval += 16
        self.streams[queue].append((waits, fn, (sembuf.sem, 16)))
        self._mark((sembuf.sem, sembuf.semval), reads, writes)

    def finish(self, queue, bufs):
        waits = self._deps(queue, bufs, bufs)
        self.streams[queue].append((waits, None, None))

    def emit(self):
        nc = self.nc
        with nc.Block() as block:
            def run(e, name):
                for waits, fn, inc in self.streams[name]:
                    for (s, v) in waits:
                        e.wait_ge(s, self.bigval() if v == BIG else v)
                    if fn is not None:
                        ins = fn(e)
                        if inc is not None:
                            ins.then_inc(inc[0], inc[1])

            @block.tensor
            def _(e):
                run(e, "pe")

            @block.scalar
            def _(e):
                run(e, "act")

            @block.vector
            def _(e):
                run(e, "dve")

            @block.gpsimd
            def _(e):
                run(e, "pool")

            @block.sync
            def _(e):
                run(e, "sp")


class WStream:
    def __init__(self, S, nc):
        self.S = S
        self.nc = nc
        self.order = []
        self.frozen = False
        self.cursor = 0
        self.passno = 0
        self.slots = []
        for i in range(NSLOT):
            t = nc.alloc_sbuf_tensor(f"wslot{i}", [128, WCOLS], BF16)
            self.slots.append(Buf(f"wslot{i}", t, S.new_sem(f"wslot{i}")))
        self.issued = -1
        self.wbf = None
        self.wf32 = None
        self.castbuf = Buf("wcast", None, S.new_sem("wcast"))
        self.nb = None
        self.npass = None

    def set_npass(self, n):
        self.npass = n

    def nblocks(self):
        return (len(self.order) + TPB - 1) // TPB

    def _issue(self, gb):
        S = self.S
        slot = self.slots[gb % NSLOT]
        me = self

        def fn(e, gb=gb, slot=slot):
            b = gb % me.nblocks()
            return e.dma_start(out=slot.t[:], in_=me.wbf[b])
        S.dma("sp", fn, reads=[self.castbuf], writes=[slot], sembuf=slot)

    def align(self, n):
        if not self.frozen:
            while len(self.order) % n:
                self.order.append(None)
            self.cursor = len(self.order)
        else:
            while self.cursor % n:
                assert self.order[self.cursor] is None
                self.cursor += 1

    def tile(self, key):
        if not self.frozen:
            self.order.append(key)
        else:
            assert self.order[self.cursor] == key, (self.order[self.cursor], key)
        pos = self.cursor
        self.cursor += 1
        blk = pos // TPB
        gb = self.passno * (self.nblocks() if self.frozen else 0) + blk
        if self.frozen:
            lim = min(gb + PREFETCH, self.npass * self.nblocks() - 1)
        else:
            lim = gb
        while self.issued < lim:
            self.issued += 1
            self._issue(self.issued)
        return self.slots[gb % NSLOT], (pos % TPB) * 128

    def end_pass(self):
        if not self.frozen:
            self.frozen = True
        assert self.cursor == len(self.order)
        self.cursor = 0
        self.passno += 1


class Prog:
    def __init__(self, stok, stage="full"):
        assert stok % T == 0
        self.stok = stok
        self.nt = stok // T
        self.stage = stage
        self.stack = contextlib.ExitStack()
        nc = self.nc = bass.Bass("TRN2", target_bir_lowering=False)
        self.S = Sched(nc, self.stack)
        self.prm_cols = {}
        self.prm_list = []
        self.rot = list(range(8))
        self.rot_i = 0
        self.rotg = [0] * NG
        self.build()

    def prm(self, name, ncols):
        if name not in self.prm_cols:
            c = sum(n for _, n in self.prm_list)
            self.prm_cols[name] = c
            self.prm_list.append((name, ncols))
            assert c + ncols <= NPRM
        return self.prm_cols[name]

    def pcol(self, name, ncols, j=0, rows=128):
        c = self.prm(name, ncols) + j
        return self.PRM.t[0:rows, c:c + 1]

    def pbc(self, name, n):
        c = self.prm(name, 8)
        return bass.AP(self.PRM.t, c, [[NPRM, 64], [1, 8], [0, n]])

    def sb(self, name, shape, dt, sem=False):
        t = self.nc.alloc_sbuf_tensor("sb_" + name, shape, dt)
        return Buf(name, t, self.S.new_sem("d_" + name) if sem else None)

    def rb(self):
        b = self.P[self.rot[self.rot_i % len(self.rot)]]
        self.rot_i += 1
        return b

    def wt(self, name, kc, mj):
        return self.W.tile((name, kc, mj))

    def mm(self, ob, oap, lb, lap, rb_, rap, start, stop, sig=None):
        self.S.op("pe", lambda e: e.matmul(oap, lhsT=lap, rhs=rap, start=start, stop=stop),
                  reads=[lb, rb_], writes=[ob], signal=stop if sig is None else sig)

    def tp(self, ob, oap, ib, iap, nrow):
        idb = self.identb
        self.S.op("pe", lambda e: e.transpose(oap, iap, idb.t[0:nrow, 0:nrow]), reads=[ib, idb], writes=[ob])

    def tt(self, eng, out, in0, in1, op, R, Wr):
        self.S.op(eng, lambda e: e.tensor_tensor(out=out, in0=in0, in1=in1, op=op), R, Wr)

    def stt(self, eng, out, in0, scalar, in1, op0, op1, R, Wr):
        self.S.op(eng, lambda e: e.scalar_tensor_tensor(out=out, in0=in0, scalar=scalar, in1=in1, op0=op0, op1=op1), R, Wr)

    def ts(self, eng, out, in0, s1, s2, op0, op1, R, Wr):
        if s2 is None:
            self.S.op(eng, lambda e: e.tensor_scalar(out=out, in0=in0, scalar1=s1, scalar2=None, op0=op0), R, Wr)
        else:
            self.S.op(eng, lambda e: e.tensor_scalar(out=out, in0=in0, scalar1=s1, scalar2=s2, op0=op0, op1=op1), R, Wr)

    def act(self, out, in_, func, R, Wr, **kw):
        self.S.op("act", lambda e: e.activation(out=out, in_=in_, func=func, **kw), R, Wr)

    def cp(self, eng, out, in_, R, Wr):
        if eng == "act":
            self.act(out, in_, AF.Copy, R, Wr)
        else:
            self.S.op(eng, lambda e: e.tensor_copy(out=out, in_=in_), R, Wr)

    def rcp(self, out, in_, R, Wr):
        self.S.op("dve", lambda e: e.reciprocal(out=out, in_=in_), R, Wr)

    def dma(self, q, out, in_, R, Wr, sb_):
        self.S.dma(q, lambda e: e.dma_start(out=out, in_=in_), R, Wr, sb_)

    def build(self):
        nc, S = self.nc, self.S
        stok = self.stok
        dt_in = lambda n, sh: nc.dram_tensor(n, sh, F32, kind="ExternalInput").ap()
        self.xT = dt_in("xT", [D, stok])
        self.outT = nc.dram_tensor("outT", [D, stok], F32, kind="ExternalOutput").ap()
        self.prm_d = dt_in("prm", [128, NPRM])
        self.rope_d = dt_in("rope", [4, 32, stok])
        self.ident_d = dt_in("ident", [128, 128])
        self.cmask_d = dt_in("cmask", [64, 3, 64])
        self.amask_d = dt_in("amask", [128, 2, T])
        self.lora_d = dt_in("lora", [128, 4, 8, 64])
        self.Kscr_d = nc.dram_tensor("Kscr", [8, 64, stok], BF16, kind="Internal").ap()
        self.Kpe_d = nc.dram_tensor("Kpescr", [32, stok], BF16, kind="Internal").ap()
        self.Vscr_d = nc.dram_tensor("Vscr", [8, 128, stok // 128, 65], BF16, kind="Internal").ap()
        self.Kscr = Buf("Kscr")
        self.Kpescr = Buf("Kpescr")
        self.Vscr = Buf("Vscr")
        self.W = WStream(S, nc)
        self.W.set_npass(self.nt)
        W = self.W
        S.bigval = lambda: 16 * W.nblocks()

        self.P = [Buf(f"ps{i}", nc.alloc_psum_tensor(f"ps{i}", [128, 512], F32)) for i in range(8)]
        for b in self.P:
            b.tb = b.t.bitcast(BF16)
        sb = self.sb
        self.PRM = sb("PRM", [128, NPRM], F32, sem=True)
        self.ones32 = sb("ones32", [128, 128], F32)
        self.identb = sb("identb", [128, 128], BF16, sem=True)
        self.cmask = sb("cmask", [64, 3, 64], BF16, sem=True)
        self.amask = sb("amask", [128, 2, T], BF16, sem=True)
        self.lora = sb("lora", [128, 4, 8, 64], BF16, sem=True)
        self.rope = sb("rope", [128, 2, T], F32, sem=True)
        self.x32 = sb("x32", [128, NKC, T], F32, sem=True)
        self.xb = sb("xb", [128, NKC, T], BF16, sem=True)
        self.h1 = sb("h1", [128, NKC, T], F32, sem=True)
        self.h1b = sb("h1b", [128, NKC, T], BF16)
        self.g = sb("g", [128, NJ, T], BF16)
        self.tmp = [sb(f"tmp{i}", [128, T], F32) for i in range(6)]
        self.mean = sb("mean", [128, T], F32)
        self.rstd = sb("rstd", [128, T], F32)
        self.mixin = sb("mixin", [128, NKC, T], BF16)
        self.qlat = sb("qlat", [128, 3, T], F32)
        self.qn = sb("qn", [128, 3, T], BF16)
        self.kvlat = sb("kvlat", [128, 2, T], F32)
        self.kvn = sb("kvn", [128, 2, T], BF16)
        self.Qall = sb("Qall", [128, 8, T], BF16)
        self.Kst = sb("Kst", [64, 8, T], BF16, sem=True)
        self.kpe = sb("kpe", [32, T], BF16, sem=True)
        self.Vst = sb("Vst", [128, 8, T // 128, 65], BF16, sem=True)
        self.Kt = [sb(f"Kt{i}", [128, KSEG], BF16, sem=True) for i in range(2)]
        self.Vt = [sb(f"Vt{i}", [128, KSEG // 128, 65], BF16, sem=True) for i in range(2)]
        self.PT = [sb(f"PT{i}", [128, T], BF16) for i in range(3)]
        self.ymla = sb("ymla", [64, 8, T], BF16)
        self.rec = sb("rec", [128, T], F32)
        self.R32 = sb("R32", [64, 8, T], F32, sem=True)
        self.K32 = sb("K32", [64, 8, T], F32)
        self.V32 = sb("V32", [64, 8, T], F32)
        self.bigtmp = sb("bigtmp", [64, 8, T], F32, sem=True)
        self.LR = sb("LR", [64, 8, 1], F32)
        self.LK = sb("LK", [64, 8, 1], F32)
        self.LV = sb("LV", [64, 8, 1], F32)
        self.WD = sb("WD", [64, T], F32)
        self.AD = sb("AD", [64, T], F32)
        self.G1 = sb("G1", [128, T], F32)
        self.G2 = sb("G2", [32, T], F32)
        self.Lsm = sb("Lsm", [128, 4], F32)
        self.tw = sb("tw", [64, T], BF16)
        self.adb = sb("adb", [64, T], BF16)
        self.sg1 = sb("sg1", [128, T], BF16)
        self.sg2 = sb("sg2", [32, T], BF16)
        self.yrw = sb("yrw", [64, 8, T], BF16)
        self.cg = []
        for gi in range(NG):
            d = {}
            for n in ("SIG", "A32", "KKN", "B32", "KM", "BON", "E", "ct0", "ct1", "ct2", "S32", "tmpS", "Y32"):
                d[n] = sb(f"{n}_{gi}", [64, HG, CH], F32)
            for n in ("AT", "RT", "KT", "BT", "KH", "BH", "VB", "GATE", "Nm0", "Nm1", "Mm0", "Mm1", "Qm0", "Qm1",
                      "AKT", "RKT", "RBT", "Vtm", "KHtm", "BHtm", "P1s", "Us", "Sb0", "Sb1"):
                d[n] = sb(f"{n}_{gi}", [64, HG, CH], BF16)
            d["GC"] = sb(f"GC_{gi}", [64, HG], F32)
            self.cg.append(d)
        self.chunk_no = 0

        self.dma("pool", self.PRM.t[:], self.prm_d, [], [self.PRM], self.PRM)
        self.dma("pool", self.identb.t[:], self.ident_d, [], [self.identb], self.identb)
        self.dma("pool", self.cmask.t[:], self.cmask_d, [], [self.cmask], self.cmask)
        self.dma("pool", self.amask.t[:], self.amask_d, [], [self.amask], self.amask)
        self.dma("pool", self.lora.t[:], self.lora_d, [], [self.lora], self.lora)
        S.op("pool", lambda e: e.memset(self.ones32.t[:], 1.0), [], [self.ones32])
        for b in [self.LR, self.LK, self.LV, self.Lsm] + [d["S32"] for d in self.cg] + [d["Sb0"] for d in self.cg]:
            S.op("pool", lambda e, b=b: e.memset(b.t[:], 0.0), [], [b])
        S.op("pool", lambda e: e.memset(self.Vst.t[:], 1.0), [], [self.Vst])
        ka = self.prm("k_a", 8)
        om = self.prm("omka", 8)
        self.ts("dve", self.PRM.t[0:64, om:om + 8], self.PRM.t[0:64, ka:ka + 8], -1.0, 1.0, ALU.mult, ALU.add, [self.PRM], [self.PRM])

        def castfn(e):
            ins = None
            for b in range(W.nblocks()):
                if b % 8 == 0 and b > 0:
                    e.wait_ge(W.castbuf.sem, 16 * b)
                ins = e.dma_start(out=W.wbf[b], in_=W.wf32[b])
                if b < W.nblocks() - 1:
                    ins.then_inc(W.castbuf.sem, 16)
            return ins
        S.streams["pool"].append(([], castfn, (W.castbuf.sem, 16)))
        W.castbuf.lw = (W.castbuf.sem, BIG)

        for it in range(self.nt):
            self.tile_body(it)
            W.end_pass()
        S.finish("pool", [self.h1, self.x32, self.R32, self.bigtmp])

        nb = W.nblocks()
        self.nb = nb
        W.wf32 = nc.dram_tensor("wf32", [nb, 128, WCOLS], F32, kind="ExternalInput").ap()
        W.wbf = nc.dram_tensor("wbf", [nb, 128, WCOLS], BF16, kind="Internal").ap()
        S.emit()

    def ln_stats(self, z, mj):
        S1, S2 = self.P[6], self.P[7]
        sq = self.tmp[mj % 2]
        self.act(sq.t[:], z.t[:, mj, :], AF.Square, [z], [sq])
        self.mm(S1, S1.t[:, 0:T], self.ones32, self.ones32.t[:], z, z.t[:, mj, :], mj == 0, mj == NKC - 1)
        self.mm(S2, S2.t[:, 0:T], self.ones32, self.ones32.t[:], sq, sq.t[:], mj == 0, mj == NKC - 1)

    def ln_apply(self, z, gname, bname, out32, outb, eps):
        S1, S2 = self.P[6], self.P[7]
        mean, rstd = self.mean, self.rstd
        self.act(mean.t[:], S1.t[:, 0:T], AF.Copy, [S1], [mean], scale=1.0 / D)
        self.tt("dve", rstd.t[:], mean.t[:], mean.t[:], ALU.mult, [mean], [rstd])
        self.stt("dve", rstd.t[:], S2.t[:, 0:T], 1.0 / D, rstd.t[:], ALU.mult, ALU.subtract, [S2, rstd], [rstd])
        self.rsqrt_(rstd, eps)
        for mj in range(NKC):
            tn = self.tmp[3 + mj % 2]
            self.tt("dve", tn.t[:], z.t[:, mj, :], mean.t[:], ALU.subtract, [z, mean], [tn])
            self.tt("dve", tn.t[:], tn.t[:], rstd.t[:], ALU.mult, [tn, rstd], [tn])
            self.act(out32.t[:, mj, :], tn.t[:], AF.Identity, [tn, self.PRM], [out32],
                     bias=self.pcol(bname, NKC, mj), scale=self.pcol(gname, NKC, mj))
            if outb is not None:
                self.act(outb.t[:, mj, :], tn.t[:], AF.Identity, [tn, self.PRM], [outb],
                         bias=self.pcol(bname, NKC, mj), scale=self.pcol(gname, NKC, mj))

    def ffn_ln(self, xin, xb, wn, gname, bname, out32, outb, after_w13=None):
        P = self.P
        for j in range(NJ):
            rows = 128 if j < NJ - 1 else DFF - 128 * (NJ - 1)
            A, B = P[(j % 2) * 2], P[(j % 2) * 2 + 1]
            for (bank, nm) in ((A, wn + "1"), (B, wn + "3")):
                for kc in range(NKC):
                    slot, off = self.wt(nm, kc, j)
                    self.mm(bank, bank.t[0:rows, 0:T], slot, slot.t[:, off:off + rows], xb, xb.t[:, kc, :], kc == 0, kc == NKC - 1)
            sl = self.tmp[j % 2]
            self.act(sl.t[0:rows, :], A.t[0:rows, 0:T], AF.Silu, [A], [sl])
            self.tt("dve", self.g.t[0:rows, j, :], sl.t[0:rows, :], B.t[0:rows, 0:T], ALU.mult, [B, sl], [self.g])
        if after_w13 is not None:
            after_w13()
        for mj in range(NKC):
            bank = P[4 + mj % 2]
            for kc in range(NJ):
                rows = 128 if kc < NJ - 1 else DFF - 128 * (NJ - 1)
                slot, off = self.wt(wn + "2", kc, mj)
                self.mm(bank, bank.t[:, 0:T], slot, slot.t[0:rows, off:off + 128], self.g, self.g.t[0:rows, kc, :], kc == 0, kc == NJ - 1)
            self.stt("dve", xin.t[:, mj, :], bank.t[:, 0:T], 0.5 / ALPHA, xin.t[:, mj, :], ALU.mult, ALU.add, [bank, xin], [xin])
            if mj > 0:
                self.ln_stats(xin, mj - 1)
        self.ln_stats(xin, NKC - 1)
        self.ln_apply(xin, gname, bname, out32, outb, LN_EPS / (ALPHA * ALPHA))

    def rms_proj(self, wname, nch, lat, outn, gname, dim):
        SS = self.P[7]
        for c in range(nch):
            bank = self.rb()
            for kc in range(NKC):
                slot, off = self.wt(wname, kc, c)
                self.mm(bank, bank.t[:, 0:T], slot, slot.t[:, off:off + 128], self.h1b, self.h1b.t[:, kc, :], kc == 0, kc == NKC - 1)
            self.cp("dve", lat.t[:, c, :], bank.t[:, 0:T], [bank], [lat])
            sq = self.tmp[c % 2]
            self.act(sq.t[:], lat.t[:, c, :], AF.Square, [lat], [sq])
            self.mm(SS, SS.t[:, 0:T], self.ones32, self.ones32.t[:], sq, sq.t[:], c == 0, c == nch - 1)
        t0, rs = self.tmp[2], self.tmp[3]
        self.ts("dve", rs.t[:], SS.t[:, 0:T], 1.0 / dim, 1e-6, ALU.mult, ALU.add, [SS], [rs])
        self.act(rs.t[:], rs.t[:], AF.Ln, [rs], [rs])
        self.act(rs.t[:], rs.t[:], AF.Exp, [rs], [rs], scale=-0.5)
        import os
        if os.environ.get("KSTT", "1") == "0":
            return
        for c in range(nch):
            self.stt("dve", outn.t[:, c, :], lat.t[:, c, :], self.pcol(gname, nch, c), rs.t[:], ALU.mult, ALU.mult, [lat, rs, self.PRM], [outn])

    def mla_tile(self, it):
        S, P = self.S, self.P
        t0 = it * T
        self.rot = [0, 1, 2, 3, 4, 5]
        self.dma("pool", self.rope.t[0:32, :, :], self.rope_d[0:2, :, t0:t0 + T].rearrange("a p t -> p a t"), [], [self.rope], self.rope)
        self.dma("pool", self.rope.t[64:96, :, :], self.rope_d[2:4, :, t0:t0 + T].rearrange("a p t -> p a t"), [], [self.rope], self.rope)
        import os
        if int(os.environ.get("KDBG", "9")) < 1:
            return
        self.rms_proj("win_q", 3, self.qlat, self.qn, "qn_g", 384.0)
        self.rms_proj("win_kv", 2, self.kvlat, self.kvn, "kvn_g", 256.0)
        import os
        if int(os.environ.get("KDBG", "9")) < 2:
            return
        bk = self.rb()
        for i, nm in enumerate(("win_kpe", "win_kpes")):
            for kc in range(NKC):
                slot, off = self.wt(nm, kc, 0)
                self.mm(bk, bk.t[0:32, i * T:(i + 1) * T], slot, slot.t[:, off:off + 32], self.h1b, self.h1b.t[:, kc, :], kc == 0, kc == NKC - 1)
        ta, tb_ = self.tmp[4], self.tmp[5]
        self.tt("dve", ta.t[0:32, :], bk.t[0:32, 0:T], self.rope.t[0:32, 0, :], ALU.mult, [bk, self.rope], [ta])
        self.tt("dve", tb_.t[0:32, :], bk.t[0:32, T:2 * T], self.rope.t[0:32, 1, :], ALU.mult, [bk, self.rope], [tb_])
        self.tt("pool", self.kpe.t[:], ta.t[0:32, :], tb_.t[0:32, :], ALU.add, [ta, tb_], [self.kpe])
        self.dma("pool", self.Kpe_d[:, t0:t0 + T], self.kpe.t[:], [self.kpe], [self.Kpescr], self.kpe)
        if int(os.environ.get("KDBG", "9")) < 3:
            return
        for h in range(8):
            bm, bs = self.rb(), self.rb()
            for (bank, nm) in ((bm, "wq_main"), (bs, "wq_swap")):
                for c in range(3):
                    slot, off = self.wt(nm, c, h)
                    self.mm(bank, bank.t[0:96, 0:T], slot, slot.t[:, off:off + 96], self.qn, self.qn.t[:, c, :], c == 0, c == 2)
            self.ts("dve", self.Qall.t[0:64, h, :], bm.t[0:64, 0:T], QSCALE, None, ALU.mult, None, [bm], [self.Qall])
            self.tt("dve", ta.t[64:96, :], bm.t[64:96, 0:T], self.rope.t[64:96, 0, :], ALU.mult, [bm, self.rope], [ta])
            self.tt("dve", tb_.t[64:96, :], bs.t[64:96, 0:T], self.rope.t[64:96, 1, :], ALU.mult, [bs, self.rope], [tb_])
            self.tt("pool", self.Qall.t[64:96, h, :], ta.t[64:96, :], tb_.t[64:96, :], ALU.add, [ta, tb_], [self.Qall])
        if int(os.environ.get("KDBG", "9")) < 4:
            return
        for h in range(8):
            bank = self.rb()
            for c in range(2):
                slot, off = self.wt("wkv_k", c, h)
                self.mm(bank, bank.t[0:64, 0:T], slot, slot.t[:, off:off + 64], self.kvn, self.kvn.t[:, c, :], c == 0, c == 1)
            self.cp("act", self.Kst.t[:, h, :], bank.t[0:64, 0:T], [bank], [self.Kst])
        self.dma("pool", self.Kscr_d[:, :, t0:t0 + T].rearrange("h r s -> r h s"), self.Kst.t[:], [self.Kst], [self.Kscr], self.Kst)
        for tb in range(T // 128):
            bank = self.rb()
            self.W.align(4)
            for c in range(2):
                sl0 = None
                for q in range(4):
                    slot, off = self.wt("wkv_v", c, q)
                    if q == 0:
                        sl0, off0 = slot, off
                    assert slot is sl0
                self.mm(bank, bank.t[:, :], self.kvn, self.kvn.t[:, c, tb * 128:(tb + 1) * 128], sl0, sl0.t[:, off0:off0 + 512], c == 0, c == 1)
            self.cp("act", self.Vst.t[:, :, tb, 0:64], bank.t[:, :].rearrange("p (h d) -> p h d", h=8), [bank], [self.Vst])
        self.dma("pool", self.Vscr_d[:, :, t0 // 128:t0 // 128 + T // 128, :].rearrange("h p k e -> p h k e"), self.Vst.t[:], [self.Vst], [self.Vscr], self.Vst)

    def attn_gen(self, it, every=1):
        P = self.P
        t0 = it * T
        nkb = (t0 + T) // 128
        SEGB = KSEG // 128
        nseg = (nkb + SEGB - 1) // SEGB
        items = [(h, sg) for h in range(8) for sg in range(nseg)]
        units = []
        for i, (h, sg) in enumerate(items):
            for kb in range(sg * SEGB, min(nkb, (sg + 1) * SEGB)):
                units.append((i, h, sg, kb))

        def load(i):
            h, sg = items[i]
            k0 = sg * KSEG
            nk = min(KSEG, t0 + T - k0)
            Kt, Vt = self.Kt[i % 2], self.Vt[i % 2]
            self.dma("pool", Kt.t[0:64, 0:nk], self.Kscr_d[h, :, k0:k0 + nk], [self.Kscr], [Kt], Kt)
            self.dma("pool", Kt.t[64:96, 0:nk], self.Kpe_d[:, k0:k0 + nk], [self.Kpescr], [Kt], Kt)
            self.dma("pool", Vt.t[:, 0:nk // 128, :], self.Vscr_d[h, :, k0 // 128:(k0 + nk) // 128, :], [self.Vscr], [Vt], Vt)

        cnt = [0]

        def bank():
            b = P[cnt[0] % 3]
            cnt[0] += 1
            return b

        def geom(u):
            i, h, sg, kb = u
            kl = kb - sg * SEGB
            loc = kb - (nkb - T // 128)
            q0 = max(loc, 0) * 128
            return kl, loc, q0

        scb = {}

        def emit_score(n):
            i, h, sg, kb = units[n]
            kl, loc, q0 = geom(units[n])
            Kt = self.Kt[i % 2]
            sc = bank()
            scb[n] = sc
            self.mm(sc, sc.t[:, q0:T], Kt, Kt.t[0:96, kl * 128:(kl + 1) * 128], self.Qall, self.Qall.t[0:96, h, q0:T], True, True)

        load(0)
        emit_score(0)
        emit_score(1)
        for n, u in enumerate(units):
            i, h, sg, kb = u
            kl, loc, q0 = geom(u)
            if kb == sg * SEGB and i + 1 < len(items):
                load(i + 1)
            if n + 2 < len(units):
                emit_score(n + 2)
            Vt = self.Vt[i % 2]
            O = P[3]
            sc = scb.pop(n)
            pt = self.PT[n % 3]
            self.act(pt.t[:, q0:T], sc.t[:, q0:T], AF.Exp, [sc], [pt])
            if loc >= 0:
                self.tt("pool", pt.t[:, q0:T], pt.t[:, q0:T], self.amask.t[:, loc, q0:T], ALU.mult, [pt, self.amask], [pt])
            self.mm(O, O.t[0:65, q0:T], Vt, Vt.t[:, kl, :], pt, pt.t[:, q0:T], kb == 0, kb == nkb - 1, sig=True)
            if kb == nkb - 1:
                rec, to, bc = self.rec, self.tmp[h % 2], sc
                self.cp("act", to.t[0:65, :], O.t[0:65, 0:T], [O], [to])
                self.rcp(rec.t[64:65, :], to.t[64:65, :], [to], [rec])
                self.mm(bc, bc.t[0:64, 0:T], self.ones32, self.ones32.t[64:65, 0:64], rec, rec.t[64:65, :], True, True)
                self.tt("dve", self.ymla.t[:, h, :], to.t[0:64, :], bc.t[0:64, 0:T], ALU.mult, [to, bc], [self.ymla])
            if (n + 1) % every == 0:
                yield

    def shift3(self, X, L, muname):
        d = self.bigtmp
        self.tt("dve", d.t[:, :, 1:T], X.t[:, :, 0:T - 1], X.t[:, :, 1:T], ALU.subtract, [X], [d])
        self.tt("pool", d.t[:, :, 0:1], L.t[:], X.t[:, :, 0:1], ALU.subtract, [X, L], [d])
        self.cp("pool", L.t[:], X.t[:, :, T - 1:T], [X], [L])
        self.tt("dve", d.t[:], d.t[:], self.pbc(muname, T), ALU.mult, [d, self.PRM], [d])
        self.tt("dve", X.t[:], X.t[:], d.t[:], ALU.add, [X, d], [X])

    def shift2(self, X, rows, lcol, muname):
        d = self.tmp[2]
        L = self.Lsm
        self.tt("dve", d.t[0:rows, 1:T], X.t[0:rows, 0:T - 1], X.t[0:rows, 1:T], ALU.subtract, [X], [d])
        self.tt("pool", d.t[0:rows, 0:1], L.t[0:rows, lcol:lcol + 1], X.t[0:rows, 0:1], ALU.subtract, [X, L], [d])
        self.cp("pool", L.t[0:rows, lcol:lcol + 1], X.t[0:rows, T - 1:T], [X], [L])
        self.stt("dve", X.t[0:rows, :], d.t[0:rows, :], self.pcol(muname, 1, 0, rows), X.t[0:rows, :], ALU.mult, ALU.add, [d, X, self.PRM], [X])

    def rwkv_tile(self, it):
        P = self.P
        self.rot = list(range(8))
        h1b = self.h1b
        for nm, X in (("win_r", self.R32), ("win_k", self.K32), ("win_v", self.V32)):
            for hp in range(4):
                bank = self.rb()
                for hh in range(2):
                    h = hp * 2 + hh
                    for kc in range(NKC):
                        slot, off = self.wt(nm, kc, h)
                        self.mm(bank, bank.t[0:64, hh * T:(hh + 1) * T], slot, slot.t[:, off:off + 64], h1b, h1b.t[:, kc, :], kc == 0, kc == NKC - 1)
                self.cp("act", X.t[:, hp * 2:hp * 2 + 2, :], bank.t[0:64, :].rearrange("p (h t) -> p h t", h=2), [bank], [X])
        bA, bB = self.rb(), self.rb()
        for (bank, lo, rows, nm, mj, X) in ((bA, 0, 64, "win_wd", 0, self.WD), (bA, T, 64, "win_ad", 0, self.AD),
                                            (bB, 0, 128, "win_gd", 0, self.G1), (bB, T, 32, "win_gd", 1, self.G2)):
            for kc in range(NKC):
                slot, off = self.wt(nm, kc, mj)
                self.mm(bank, bank.t[0:rows, lo:lo + T], slot, slot.t[:, off:off + rows], h1b, h1b.t[:, kc, :], kc == 0, kc == NKC - 1)
            self.cp("act", X.t[0:rows, :], bank.t[0:rows, lo:lo + T], [bank], [X])
        self.shift3(self.R32, self.LR, "mu_r")
        self.shift3(self.K32, self.LK, "mu_k")
        self.shift3(self.V32, self.LV, "mu_v")
        self.shift2(self.WD, 64, 0, "mu_wd")
        self.shift2(self.AD, 64, 1, "mu_ad")
        self.shift2(self.G1, 128, 2, "mu_g1")
        self.shift2(self.G2, 32, 3, "mu_g2")
        self.act(self.tw.t[:], self.WD.t[:], AF.Tanh, [self.WD], [self.tw])
        self.cp("pool", self.adb.t[:], self.AD.t[:], [self.AD], [self.adb])
        self.act(self.sg1.t[:], self.G1.t[:], AF.Sigmoid, [self.G1], [self.sg1])
        self.act(self.sg2.t[:], self.G2.t[:], AF.Sigmoid, [self.G2], [self.sg2])

    def rwkv_chunks_gen(self):
        for ci in range(T // CH):
            yield from inter([self.rwkv_chunk(ci, gi) for gi in range(NG)])
        self.chunk_no += T // CH

    def pbg(self, name, gi, n):
        c = self.prm(name, 8) + gi * HG
        return bass.AP(self.PRM.t, c, [[NPRM, 64], [1, HG], [0, n]])

    def rbg(self, gi):
        nb = 4 // NG
        b = self.P[4 + gi * nb + self.rotg[gi] % nb]
        self.rotg[gi] += 1
        return b

    def rsqrt_(self, buf, eps):
        self.ts("dve", buf.t[:], buf.t[:], 0.0, eps, ALU.max, ALU.add, [buf], [buf])
        self.act(buf.t[:], buf.t[:], AF.Ln, [buf], [buf])
        self.act(buf.t[:], buf.t[:], AF.Exp, [buf], [buf], scale=-0.5)

    def rwkv_chunk(self, ci, gi):
        cs = slice(ci * CH, (ci + 1) * CH)
        h0 = gi * HG
        G = self.cg[gi]
        Rc, Kc, Vc = self.R32.t[:, h0:h0 + HG, cs], self.K32.t[:, h0:h0 + HG, cs], self.V32.t[:, h0:h0 + HG, cs]
        RR, KK, VV = [self.R32], [self.K32], [self.V32]
        PR = self.PRM
        lora = self.lora
        SIG, A32, KKN, B32, KM, BON, E = [G[n] for n in ("SIG", "A32", "KKN", "B32", "KM", "BON", "E")]
        ct0, ct1, ct2 = G["ct0"], G["ct1"], G["ct2"]
        AT, RT, KT, BT, KH, BH, VB, GATE = [G[n] for n in ("AT", "RT", "KT", "BT", "KH", "BH", "VB", "GATE")]
        W_ = HG * CH
        v8 = lambda bank: bank.t[0:64, 0:W_].rearrange("p (h t) -> p h t", h=HG)
        v8b = lambda bank: bank.tb[0:64, 0:W_].rearrange("p (h t) -> p h t", h=HG)
        flat = lambda b: b.t[:].rearrange("p h t -> p (h t)")
        ones64 = self.ones32.t[0:64, 0:64]
        rb = lambda: self.rbg(gi)
        bc = lambda name: self.pbg(name, gi, CH)
        b1 = rb()
        for h in range(HG):
            self.mm(b1, b1.t[0:64, h * CH:(h + 1) * CH], lora, lora.t[0:64, 0, h0 + h, :], self.tw, self.tw.t[:, cs], True, True)
        yield
        self.tt("dve", SIG.t[:], v8(b1), bc("w0"), ALU.add, [b1, PR], [SIG])
        yield
        self.act(SIG.t[:], SIG.t[:], AF.Exp, [SIG], [SIG], scale=-1.0)
        self.act(SIG.t[:], SIG.t[:], AF.Ln, [SIG], [SIG], bias=1.0)
        self.act(SIG.t[:], SIG.t[:], AF.Exp, [SIG], [SIG], scale=-1.0)
        yield
        b2 = rb()
        for h in range(HG):
            self.mm(b2, b2.t[0:64, h * CH:(h + 1) * CH], lora, lora.t[0:64, 1, h0 + h, :], self.adb, self.adb.t[:, cs], True, True)
        yield
        self.tt("dve", A32.t[:], v8(b2), bc("a0"), ALU.add, [b2, PR], [A32])
        yield
        self.act(A32.t[:], A32.t[:], AF.Exp, [A32], [A32], scale=-1.0)
        self.act(A32.t[:], A32.t[:], AF.Ln, [A32], [A32], bias=1.0)
        self.act(A32.t[:], A32.t[:], AF.Exp, [A32], [A32], scale=-1.0)
        yield
        b3 = rb()
        for h in range(HG):
            self.mm(b3, b3.t[0:64, h * CH:(h + 1) * CH], lora, lora.t[:, 2, h0 + h, :], self.sg1, self.sg1.t[:, cs], True, False)
            self.mm(b3, b3.t[0:64, h * CH:(h + 1) * CH], lora, lora.t[0:32, 3, h0 + h, :], self.sg2, self.sg2.t[0:32, cs], False, True)
        yield
        self.cp("act", GATE.t[:], v8(b3), [b3], [GATE])
        yield
        self.tt("dve", KKN.t[:], Kc, bc("k_k"), ALU.mult, KK + [PR], [KKN])
        yield
        self.act(ct0.t[:], KKN.t[:], AF.Square, [KKN], [ct0])
        yield
        b4 = rb()
        self.mm(b4, b4.t[0:64, 0:W_], self.ones32, ones64, ct0, flat(ct0), True, True)
        yield
        self.cp("dve", ct1.t[:], v8(b4), [b4], [ct1])
        self.rsqrt_(ct1, 1e-24)
        yield
        self.tt("dve", KKN.t[:], KKN.t[:], ct1.t[:], ALU.mult, [KKN, ct1], [KKN])
        yield
        self.tt("dve", ct0.t[:], A32.t[:], bc("k_a"), ALU.mult, [A32, PR], [ct0])
        yield
        self.tt("dve", ct0.t[:], ct0.t[:], bc("omka"), ALU.add, [ct0, PR], [ct0])
        yield
        self.tt("dve", KM.t[:], Kc, ct0.t[:], ALU.mult, KK + [ct0], [KM])
        yield
        self.tt("pool", B32.t[:], KKN.t[:], A32.t[:], ALU.mult, [KKN, A32], [B32])
        yield
        self.tt("dve", ct0.t[:], Rc, KM.t[:], ALU.mult, RR + [KM], [ct0])
        yield
        self.tt("dve", ct0.t[:], ct0.t[:], bc("r_k"), ALU.mult, [ct0, PR], [ct0])
        yield
        b5 = rb()
        self.mm(b5, b5.t[0:64, 0:W_], self.ones32, ones64, ct0, flat(ct0), True, True)
        yield
        self.tt("dve", BON.t[:], v8(b5), Vc, ALU.mult, [b5] + VV, [BON])
        yield
        src, dst = SIG, ct2
        for d in (1, 2, 4, 8, 16, 32):
            self.tt("dve", dst.t[:, :, d:CH], src.t[:, :, d:CH], src.t[:, :, 0:CH - d], ALU.add, [src], [dst])
            self.cp("pool", dst.t[:, :, 0:d], src.t[:, :, 0:d], [src], [dst])
            src, dst = dst, src
            yield
        assert src is SIG
        self.act(E.t[:], SIG.t[:], AF.Exp, [SIG], [E], scale=-C0)
        yield
        self.tt("dve", RT.t[:], Rc, E.t[:], ALU.mult, RR + [E], [RT])
        yield
        self.stt("dve", AT.t[:, :, 1:CH], KKN.t[:, :, 1:CH], -1.0, E.t[:, :, 0:CH - 1], ALU.mult, ALU.mult, [KKN, E], [AT])
        yield
        self.ts("pool", AT.t[:, :, 0:1], KKN.t[:, :, 0:1], -1.0, None, ALU.mult, None, [KKN], [AT])
        self.cp("pool", G["GC"].t[:, :], E.t[:, :, CH - 1], [E], [G["GC"]])
        yield
        self.act(E.t[:], SIG.t[:], AF.Exp, [SIG], [E], scale=C0)
        yield
        self.tt("dve", KT.t[:], KM.t[:], E.t[:], ALU.mult, [KM, E], [KT])
        yield
        self.tt("pool", BT.t[:], B32.t[:], E.t[:], ALU.mult, [B32, E], [BT])
        yield
        last_bc = bass.AP(SIG.t, CH - 1, [[HG * CH, 64], [CH, HG], [0, CH]])
        self.tt("dve", ct0.t[:], last_bc, SIG.t[:], ALU.subtract, [SIG], [ct0])
        yield
        self.act(E.t[:], ct0.t[:], AF.Exp, [ct0], [E], scale=-C0)
        yield
        self.tt("dve", KH.t[:], KM.t[:], E.t[:], ALU.mult, [KM, E], [KH])
        yield
        self.tt("pool", BH.t[:], B32.t[:], E.t[:], ALU.mult, [B32, E], [BH])
        yield
        self.cp("act", VB.t[:], Vc, VV, [VB])
        yield
        cm = self.cmask
        mbc = lambda i: bass.AP(cm.t, i * 64, [[3 * 64, 64], [0, HG], [1, 64]])
        N0, M0, Q0 = G["Nm0"], G["Mm0"], G["Qm0"]
        for (L, R, dstb, mi) in ((AT, BT, N0, 0), (BT, AT, M0, 1), (KT, AT, G["AKT"], 1),
                                 (KT, RT, G["RKT"], 2), (BT, RT, G["RBT"], 2)):
            bank = rb()
            for h in range(HG):
                self.mm(bank, bank.t[0:64, h * CH:(h + 1) * CH], L, L.t[:, h, :], R, R.t[:, h, :], True, True)
            yield
            self.tt("dve", dstb.t[:], v8(bank), mbc(mi), ALU.mult, [bank, cm], [dstb])
            yield
        idbc = bass.AP(self.identb.t, 0, [[128, 64], [0, HG], [1, 64]])
        self.tt("dve", Q0.t[:], M0.t[:], idbc, ALU.add, [M0, self.identb], [Q0])
        yield
        cur = 0
        pend = None
        for j in range(1, 6):
            No, Mo = G[f"Nm{cur}"], G[f"Mm{cur}"]
            Nn, Mn = G[f"Nm{1 - cur}"], G[f"Mm{1 - cur}"]
            bn = rb()
            for h in range(HG):
                self.mm(bn, bn.t[0:64, h * CH:(h + 1) * CH], Mo, Mo.t[:, h, :], No, No.t[:, h, :], True, True)
            yield
            self.cp("act", Nn.t[:], v8(bn), [bn], [Nn])
            yield
            if j < 5:
                bm_ = rb()
                for h in range(HG):
                    self.mm(bm_, bm_.t[0:64, h * CH:(h + 1) * CH], No, No.t[:, h, :], Mo, Mo.t[:, h, :], True, True)
                yield
                self.cp("dve", Mn.t[:], v8(bm_), [bm_], [Mn])
                yield
            Qo, Qn = G[f"Qm{(j - 1) % 2}"], G[f"Qm{j % 2}"]
            bq = rb()
            for h in range(HG):
                self.mm(bq, bq.t[0:64, h * CH:(h + 1) * CH], Nn, Nn.t[:, h, :], Qo, Qo.t[:, h, :], True, True)
            yield
            self.tt("dve", Qn.t[:], v8(bq), Qo.t[:], ALU.add, [bq, Qo], [Qn])
            yield
            cur = 1 - cur
        cur = 5 % 2
        Qf = G[f"Qm{cur}"]
        for (X, Xtm) in ((VB, G["Vtm"]), (KH, G["KHtm"]), (BH, G["BHtm"])):
            bank = rb()
            for h in range(HG):
                self.tp(bank, bank.tb[0:64, h * CH:(h + 1) * CH], X, X.t[:, h, :], 64)
            yield
            self.cp("act", Xtm.t[:], v8b(bank), [bank], [Xtm])
            yield
        cn = self.chunk_no + ci
        Sb_cur, Sb_nxt = G[f"Sb{cn % 2}"], G[f"Sb{(cn + 1) % 2}"]
        S32, tmpS, Y, P1s, Us, Vtm = G["S32"], G["tmpS"], G["Y32"], G["P1s"], G["Us"], G["Vtm"]
        bp = rb()
        for h in range(HG):
            o = bp.t[0:64, h * CH:(h + 1) * CH]
            self.mm(bp, o, G["AKT"], G["AKT"].t[:, h, :], Vtm, Vtm.t[:, h, :], True, False)
            self.mm(bp, o, AT, AT.t[:, h, :], Sb_cur, Sb_cur.t[:, h, :], False, True)
        yield
        self.cp("act", P1s.t[:], v8(bp), [bp], [P1s])
        yield
        bu = rb()
        for h in range(HG):
            self.mm(bu, bu.t[0:64, h * CH:(h + 1) * CH], Qf, Qf.t[:, h, :], P1s, P1s.t[:, h, :], True, True)
        yield
        self.cp("dve", Us.t[:], v8(bu), [bu], [Us])
        yield
        bd = rb()
        for h in range(HG):
            o = bd.t[0:64, h * CH:(h + 1) * CH]
            self.mm(bd, o, G["KHtm"], G["KHtm"].t[:, h, :], Vtm, Vtm.t[:, h, :], True, False)
            self.mm(bd, o, G["BHtm"], G["BHtm"].t[:, h, :], Us, Us.t[:, h, :], False, True)
        by = rb()
        for h in range(HG):
            o = by.t[0:64, h * CH:(h + 1) * CH]
            self.mm(by, o, Sb_cur, Sb_cur.t[:, h, :], RT, RT.t[:, h, :], True, False)
            self.mm(by, o, Vtm, Vtm.t[:, h, :], G["RKT"], G["RKT"].t[:, h, :], False, False)
            self.mm(by, o, Us, Us.t[:, h, :], G["RBT"], G["RBT"].t[:, h, :], False, True)
        yield
        gcb = bass.AP(G["GC"].t, 0, [[HG, 64], [1, HG], [0, CH]])
        self.tt("pool", tmpS.t[:], S32.t[:], gcb, ALU.mult, [S32, G["GC"]], [tmpS])
        yield
        self.tt("dve", S32.t[:], tmpS.t[:], v8(bd), ALU.add, [tmpS, bd], [S32])
        yield
        self.cp("act", Sb_nxt.t[:], S32.t[:], [S32], [Sb_nxt])
        self.cp("act", Y.t[:], v8(by), [by], [Y])
        yield
        self.act(ct0.t[:], Y.t[:], AF.Square, [Y], [ct0])
        s1, s2 = rb(), rb()
        self.mm(s1, s1.t[0:64, 0:W_], self.ones32, ones64, Y, flat(Y), True, True)
        yield
        self.mm(s2, s2.t[0:64, 0:W_], self.ones32, ones64, ct0, flat(ct0), True, True)
        self.act(ct1.t[:], v8(s1), AF.Copy, [s1], [ct1], scale=1.0 / 64)
        yield
        self.tt("dve", ct2.t[:], ct1.t[:], ct1.t[:], ALU.mult, [ct1], [ct2])
        yield
        self.stt("dve", ct2.t[:], v8(s2), 1.0 / 64, ct2.t[:], ALU.mult, ALU.subtract, [s2, ct2], [ct2])
        self.rsqrt_(ct2, GN_EPS)
        yield
        self.tt("dve", ct0.t[:], Y.t[:], ct1.t[:], ALU.subtract, [Y, ct1], [ct0])
        yield
        self.tt("dve", ct0.t[:], ct0.t[:], ct2.t[:], ALU.mult, [ct0, ct2], [ct0])
        yield
        self.tt("dve", ct0.t[:], ct0.t[:], bc("gn_g"), ALU.mult, [ct0, PR], [ct0])
        yield
        self.tt("dve", ct0.t[:], ct0.t[:], bc("gn_b"), ALU.add, [ct0, PR], [ct0])
        yield
        self.tt("dve", ct0.t[:], ct0.t[:], BON.t[:], ALU.add, [ct0, BON], [ct0])
        yield
        self.tt("dve", self.yrw.t[:, h0:h0 + HG, cs], ct0.t[:], GATE.t[:], ALU.mult, [ct0, GATE], [self.yrw])
        yield

    def mix_tile(self):
        P = self.P
        h1, h1b = self.h1, self.h1b
        for mj in range(NKC):
            parts = []
            for (yb, wn, gn, tmpi) in ((self.yrw, "w_up_rwkv", "win_gr", 0), (self.ymla, "w_up_mla", "win_gm", 1)):
                by = P[tmpi * 2 + 4 * (mj % 2)]
                bg = P[tmpi * 2 + 1 + 4 * (mj % 2)]
                for h in range(8):
                    slot, off = self.wt(wn, h, mj)
                    self.mm(by, by.t[:, 0:T], slot, slot.t[0:64, off:off + 128], yb, yb.t[:, h, :], h == 0, h == 7)
                for kc in range(NKC):
                    slot, off = self.wt(gn, kc, mj)
                    self.mm(bg, bg.t[:, 0:T], slot, slot.t[:, off:off + 128], h1b, h1b.t[:, kc, :], kc == 0, kc == NKC - 1)
                sg = self.tmp[tmpi + 2 * (mj % 2)]
                self.act(sg.t[:], bg.t[:, 0:T], AF.Sigmoid, [bg], [sg])
                self.tt("dve", sg.t[:], sg.t[:], by.t[:, 0:T], ALU.mult, [sg, by], [sg])
                parts.append(sg)
            self.tt("pool", self.mixin.t[:, mj, :], parts[0].t[:], parts[1].t[:], ALU.add, parts, [self.mixin])
        for mo in range(NKC):
            bank = P[mo % 4]
            for mj in range(NKC):
                slot, off = self.wt("w_o", mj, mo)
                self.mm(bank, bank.t[:, 0:T], slot, slot.t[:, off:off + 128], self.mixin, self.mixin.t[:, mj, :], mj == 0, mj == NKC - 1)
            self.stt("dve", h1.t[:, mo, :], bank.t[:, 0:T], 1.0 / ALPHA, h1.t[:, mo, :], ALU.mult, ALU.add, [bank, h1], [h1])
            if mo > 0:
                self.ln_stats(h1, mo - 1)
        self.ln_stats(h1, NKC - 1)
        self.ln_apply(h1, "ln2_g", "ln2_b", self.x32, self.xb, LN_EPS / (ALPHA * ALPHA))

    def store(self, src, nrow_chunks, t0, rows=128):
        dst = self.outT.rearrange("(kc p) s -> p kc s", p=128)[0:rows, 0:nrow_chunks, t0:t0 + T]
        self.dma("pool", dst, src.t[:], [src], [], src)

    def load_xb(self, it):
        src = self.xT.rearrange("(kc p) s -> p kc s", p=128)[:, :, it * T:(it + 1) * T]
        self.dma("pool", self.xb.t[:], src, [], [self.xb], self.xb)

    def tile_body(self, it):
        S = self.S
        t0 = it * T
        x32 = self.x32
        xsrc = self.xT.rearrange("(kc p) s -> p kc s", p=128)[:, :, t0:t0 + T]
        self.dma("sp", x32.t[:], xsrc, [], [x32], x32)
        if it == 0:
            self.load_xb(0)
        self.ffn_ln(x32, self.xb, "ffn1_w", "ln1_g", "ln1_b", self.h1, self.h1b)
        if self.stage == "ffn1":
            return self.store(self.h1, NKC, t0)
        if self.stage in ("mla", "full", "h2"):
            self.mla_tile(it)
            if self.stage == "mla":
                interleave([self.attn_gen(it)])
        if self.stage == "mla":
            dst = self.outT[0:512, t0:t0 + T].rearrange("(h p) s -> p h s", p=64)
            self.S.op("act", lambda e: e.activation(out=self.R32.t[:], in_=self.ymla.t[:], func=AF.Copy), [self.ymla], [self.R32])
            self.dma("sp", dst, self.R32.t[:], [self.R32], [], self.R32)
            return
        if self.stage in ("rwkv", "full", "h2"):
            self.rwkv_tile(it)
            if self.stage == "rwkv":
                interleave([self.rwkv_chunks_gen()])
            else:
                nA = 8 * ((t0 + T) // 128)
                nB = 75 * (T // CH)
                interleave([self.attn_gen(it, every=max(1, round(nA / nB))), self.rwkv_chunks_gen()])
        if self.stage == "rwkv":
            dst = self.outT[0:512, t0:t0 + T].rearrange("(h p) s -> p h s", p=64)
            self.S.op("act", lambda e: e.activation(out=self.bigtmp.t[:], in_=self.yrw.t[:], func=AF.Copy), [self.yrw], [self.bigtmp])
            self.dma("sp", dst, self.bigtmp.t[:], [self.bigtmp], [], self.bigtmp)
            return
        self.mix_tile()
        if self.stage == "h2":
            return self.store(self.x32, NKC, t0)
        self.ffn_ln(self.x32, self.xb, "ffn2_w", "ln3_g", "ln3_b", self.h1, None,
                    after_w13=(lambda: self.load_xb(it + 1)) if it + 1 < self.nt else None)
        self.store(self.h1, NKC, t0)


_PROG_CACHE = {}


def get_prog(stok, stage):
    k = (stok, stage)
    if k not in _PROG_CACHE:
        _PROG_CACHE[k] = Prog(stok, stage)
    return _PROG_CACHE[k]


def pack_weights(prog, mats):
    order = prog.W.order
    nb = prog.nb
    wf = np.zeros((nb, 128, WCOLS), np.float32)
    for pos, key in enumerate(order):
        if key is None:
            continue
        name, kc, mj = key
        blk = mats[name][kc * 128:(kc + 1) * 128, mj * 128:(mj + 1) * 128]
        b, o = pos // TPB, (pos % TPB) * 128
        wf[b, :blk.shape[0], o:o + blk.shape[1]] = blk
    return wf


def fm(v):
    return np.ascontiguousarray(np.asarray(v, np.float32).reshape(-1, 128).T)


def h8(v):
    return np.ascontiguousarray(np.asarray(v, np.float32).reshape(8, 64).T)


def headpad_cols(w, per):
    K = w.shape[0]
    out = np.zeros((K, 8 * 128), np.float32)
    for h in range(8):
        out[:, h * 128:h * 128 + per] = w[:, h * per:(h + 1) * per]
    return out


def headpad_rows(w, per):
    M = w.shape[1]
    out = np.zeros((8 * 128, M), np.float32)
    for h in range(8):
        out[h * 128:h * 128 + per, :] = w[h * per:(h + 1) * per, :]
    return out


def host_constants(stok):
    inv = (10000.0 ** (-np.arange(0, 32, 2, dtype=np.float32) / 32)).astype(np.float32)
    ang = np.arange(stok, dtype=np.float32)[None, :] * inv[:, None]
    cos, sin = np.cos(ang), np.sin(ang)
    cosF = np.concatenate([cos, cos], 0)
    sinS = np.concatenate([-sin, sin], 0)
    rope = np.stack([cosF, sinS, cosF * QSCALE, sinS * QSCALE], 0).astype(np.float32)
    ident = np.eye(128, dtype=np.float32)
    i = np.arange(64)
    cmask = np.zeros((64, 3, 64), np.float32)
    cmask[:, 0, :] = (i[None, :] < i[:, None])
    cmask[:, 1, :] = (i[:, None] < i[None, :])
    cmask[:, 2, :] = (i[:, None] <= i[None, :])
    amask = np.zeros((128, T // 128, T), np.float32)
    p = np.arange(128)
    q = np.arange(T)
    for kl in range(T // 128):
        amask[:, kl, :] = ((kl * 128 + p)[:, None] // 64) <= (q[None, :] // 64)
    return rope, ident, cmask, amask


def run(inputs, stok=None, stage="full", ncores=8):
    x = np.asarray(inputs["x"], np.float32)
    if stok is None:
        stok = x.shape[1]
    prog = get_prog(stok, stage)
    L = 0
    g = lambda n: np.asarray(inputs[n][L], np.float32)
    w_in = g("w_in")
    c = 0
    cols = {}
    for nm, n in (("r", 512), ("k", 512), ("v", 512), ("wd", 64), ("ad", 64), ("gd", 160), ("q", 384), ("kv", 256), ("kpe", 32), ("gr", 1024), ("gm", 1024)):
        cols[nm] = w_in[:, c:c + n]
        c += n
    wq = g("w_q_up").reshape(384, 8, 96)
    wq_main = wq.reshape(384, 768)
    wq_swap = np.concatenate([wq[:, :, 0:64], wq[:, :, 80:96], wq[:, :, 64:80]], axis=2).reshape(384, 768)
    wkv = g("w_kv_up").reshape(256, 8, 128)
    mats = {
        "ffn1_w1": g("ffn1_w1"), "ffn1_w3": g("ffn1_w3"), "ffn1_w2": g("ffn1_w2"),
        "ffn2_w1": g("ffn2_w1"), "ffn2_w3": g("ffn2_w3"), "ffn2_w2": g("ffn2_w2"),
        "win_q": cols["q"], "win_kv": cols["kv"], "win_kpe": cols["kpe"],
        "win_kpes": np.concatenate([cols["kpe"][:, 16:32], cols["kpe"][:, 0:16]], 1),
        "wq_main": headpad_cols(wq_main, 96), "wq_swap": headpad_cols(wq_swap, 96),
        "wkv_k": headpad_cols(np.ascontiguousarray(wkv[:, :, 0:64]).reshape(256, 512), 64),
        "wkv_v": np.ascontiguousarray(wkv[:, :, 64:128]).reshape(256, 512),
        "win_r": headpad_cols(cols["r"], 64), "win_k": headpad_cols(cols["k"], 64), "win_v": headpad_cols(cols["v"], 64),
        "win_wd": cols["wd"], "win_ad": cols["ad"], "win_gd": cols["gd"],
        "win_gr": cols["gr"], "win_gm": cols["gm"],
        "w_up_rwkv": headpad_rows(g("w_up_rwkv"), 64), "w_up_mla": headpad_rows(g("w_up_mla"), 64),
        "w_o": g("w_o"),
    }
    mu = g("mu_shift")
    vecs = {
        "ln1_g": fm(g("ln1_g")), "ln1_b": fm(g("ln1_b")), "ln2_g": fm(g("ln2_g")), "ln2_b": fm(g("ln2_b")),
        "ln3_g": fm(g("ln3_g")), "ln3_b": fm(g("ln3_b")),
        "qn_g": fm(g("q_norm_g")), "kvn_g": fm(g("kv_norm_g")),
        "mu_r": h8(mu[0:512]), "mu_k": h8(mu[512:1024]), "mu_v": h8(mu[1024:1536]),
        "mu_wd": mu[1536:1600, None], "mu_ad": mu[1600:1664, None], "mu_g1": mu[1664:1792, None], "mu_g2": mu[1792:1824, None],
        "w0": h8(g("w0")), "a0": h8(g("a0")), "k_k": h8(g("k_k")), "k_a": h8(g("k_a")), "r_k": h8(g("r_k").reshape(-1)),
        "gn_g": h8(g("gn_g")), "gn_b": h8(g("gn_b")),
    }
    prm = np.zeros((128, NPRM), np.float32)
    for name, n in prog.prm_list:
        if name in vecs:
            v = vecs[name]
            prm[:v.shape[0], prog.prm_cols[name]:prog.prm_cols[name] + n] = v
    lora = np.zeros((128, 4, 8, 64), np.float32)
    lora[0:64, 0] = g("w_decay_up").reshape(64, 8, 64)
    lora[0:64, 1] = g("w_iclr_up").reshape(64, 8, 64)
    wg = g("w_gate_up").reshape(160, 8, 64)
    lora[:, 2] = wg[0:128]
    lora[0:32, 3] = wg[128:160]
    wf = pack_weights(prog, mats)
    rope, ident, cmask, amask = host_constants(stok)
    in_maps = []
    for c in range(ncores):
        xT = np.ascontiguousarray(x[c, :stok, :].T)
        in_maps.append({"xT": xT, "prm": prm, "wf32": wf, "rope": rope, "ident": ident, "cmask": cmask, "amask": amask, "lora": lora})
    res = run_bass_kernel_spmd(prog.nc, in_maps, core_ids=list(range(ncores)))
    out = np.stack([np.ascontiguousarray(res.results[c]["outT"].T) for c in range(ncores)], axis=0)
    return out


def kernel(**inputs):
    return run(inputs).astype(np.float32)
```

```python
import contextlib
import numpy as np
import ml_dtypes
import concourse.bass as bass
import concourse.mybir as mybir
from concourse.bass_utils import run_bass_kernel_spmd

F32 = mybir.dt.float32
BF16 = mybir.dt.bfloat16
AF = mybir.ActivationFunctionType
ALU = mybir.AluOpType

D = 1024
DFF = 2752
T = 256
NKC = D // 128
NJ = (DFF + 127) // 128
ALPHA = 2.0 ** 0.25
LN_EPS = 1e-5
WCOLS = 2048
TPB = WCOLS // 128
NSLOT = 6
PREFETCH = 4
NPRM = 160
KSEG = 1024
CH = 64
QSCALE = 96.0 ** -0.5
C0 = float(np.exp(-0.5))
GN_EPS = 64e-5
CASTG = 8
NG = 2
HG = 8 // NG


def inter(gens):
    gens = list(gens)
    while gens:
        for g_ in list(gens):
            try:
                next(g_)
            except StopIteration:
                gens.remove(g_)
        yield


def interleave(gens):
    for _ in inter(gens):
        pass


ENGS = ("pe", "act", "dve", "pool", "sp")
BIG = 1 << 30


class Buf:
    def __init__(self, name, t=None, sem=None):
        self.name = name
        self.t = t
        self.sem = sem
        self.semval = 0
        self.lw = None
        self.rd = {}


class Sched:
    def __init__(self, nc, stack):
        self.nc = nc
        self.stack = stack
        self.streams = {e: [] for e in ENGS}
        self.esem = {}
        for e in ("pe", "act", "dve", "pool"):
            self.esem[e] = stack.enter_context(nc.semaphore("es_" + e))
        self.ecnt = {e: 0 for e in ENGS}
        self.waited = {e: {} for e in ENGS}
        self.nsem = 4
        self.bigval = None

    def new_sem(self, name):
        self.nsem += 1
        return self.stack.enter_context(self.nc.semaphore(name))

    def _deps(self, eng, reads, writes):
        need = {}

        def add(sv):
            if sv is None:
                return
            s, v = sv
            k = id(s)
            if k not in need or need[k][1] < v:
                need[k] = (s, v)
        for b in reads:
            add(b.lw)
        for b in writes:
            add(b.lw)
            for k, sv in b.rd.items():
                add(sv)
        waits = []
        w = self.waited[eng]
        for k, (s, v) in need.items():
            if eng == "pe" and s is self.esem["pe"]:
                continue
            if w.get(k, 0) < v:
                w[k] = v
                waits.append((s, v))
        return waits

    def _mark(self, tag, reads, writes):
        for b in reads:
            k = id(tag[0])
            if k not in b.rd or b.rd[k][1] < tag[1]:
                b.rd[k] = tag
        for b in writes:
            b.lw = tag
            b.rd = {}

    def op(self, eng, fn, reads=(), writes=(), signal=True):
        waits = self._deps(eng, reads, writes)
        s = self.esem[eng]
        idx = self.ecnt[eng] + 1
        if signal:
            self.ecnt[eng] = idx
        self.streams[eng].append((waits, fn, (s, 1) if signal else None))
        self._mark((s, idx), reads, writes)

    def dma(self, queue, fn, reads, writes, sembuf):
        waits = self._deps(queue, reads, writes)
        sembuf.semval += 16
        self.streams[queue].append((waits, fn, (sembuf.sem, 16)))
        self._mark((sembuf.sem, sembuf.semval), reads, writes)

    def finish(self, queue, bufs):
        waits = self._deps(queue, bufs, bufs)
        self.streams[queue].append((waits, None, None))

    def emit(self):
        nc = self.nc
        with nc.Block() as block:
            def run(e, name):
                for waits, fn, inc in self.streams[name]:
                    for (s, v) in waits:
                        e.wait_ge(s, self.bigval(v - BIG) if v >= BIG else v)
                    if fn is not None:
                        ins = fn(e)
                        if inc is not None:
                            ins.then_inc(inc[0], inc[1])

            @block.tensor
            def _(e):
                run(e, "pe")

            @block.scalar
            def _(e):
                run(e, "act")

            @block.vector
            def _(e):
                run(e, "dve")

            @block.gpsimd
            def _(e):
                run(e, "pool")

            @block.sync
            def _(e):
                run(e, "sp")


class WStream:
    def __init__(self, S, nc):
        self.S = S
        self.nc = nc
        self.order = []
        self.frozen = False
        self.cursor = 0
        self.passno = 0
        self.slots = []
        for i in range(NSLOT):
            t = nc.alloc_sbuf_tensor(f"wslot{i}", [128, WCOLS], BF16)
            self.slots.append(Buf(f"wslot{i}", t, S.new_sem(f"wslot{i}")))
        self.issued = -1
        self.wbf = None
        self.wf32 = None
        self.castbuf = Buf("wcast", None, S.new_sem("wcast"))
        self.castgrp = []
        self.nb = None
        self.npass = None

    def set_npass(self, n):
        self.npass = n

    def nblocks(self):
        return (len(self.order) + TPB - 1) // TPB

    def _issue(self, gb):
        S = self.S
        slot = self.slots[gb % NSLOT]
        me = self

        def fn(e, gb=gb, slot=slot):
            b = gb % me.nblocks()
            return e.dma_start(out=slot.t[:], in_=me.wbf[b])
        if not self.frozen:
            g = gb // CASTG
            while len(self.castgrp) <= g:
                cb = Buf(f"wcast{len(self.castgrp)}")
                cb.lw = (self.castbuf.sem, BIG + len(self.castgrp))
                self.castgrp.append(cb)
            dep = self.castgrp[g]
        else:
            dep = self.castbuf
        S.dma("sp", fn, reads=[dep], writes=[slot], sembuf=slot)

    def align(self, n):
        if not self.frozen:
            while len(self.order) % n:
                self.order.append(None)
            self.cursor = len(self.order)
        else:
            while self.cursor % n:
                assert self.order[self.cursor] is None
                self.cursor += 1

    def tile(self, key):
        if not self.frozen:
            self.order.append(key)
        else:
            assert self.order[self.cursor] == key, (self.order[self.cursor], key)
        pos = self.cursor
        self.cursor += 1
        blk = pos // TPB
        gb = self.passno * (self.nblocks() if self.frozen else 0) + blk
        if self.frozen:
            lim = min(gb + PREFETCH, self.npass * self.nblocks() - 1)
        else:
            lim = gb
        while self.issued < lim:
            self.issued += 1
            self._issue(self.issued)
        return self.slots[gb % NSLOT], (pos % TPB) * 128

    def end_pass(self):
        if not self.frozen:
            self.frozen = True
        assert self.cursor == len(self.order)
        self.cursor = 0
        self.passno += 1


class Prog:
    def __init__(self, stok, stage="full"):
        assert stok % T == 0
        self.stok = stok
        self.nt = stok // T
        self.stage = stage
        self.stack = contextlib.ExitStack()
        nc = self.nc = bass.Bass("TRN2", target_bir_lowering=False)
        self.S = Sched(nc, self.stack)
        self.prm_cols = {}
        self.prm_list = []
        self.rot = list(range(8))
        self.rot_i = 0
        self.rotg = [0] * NG
        self.build()

    def prm(self, name, ncols):
        if name not in self.prm_cols:
            c = sum(n for _, n in self.prm_list)
            self.prm_cols[name] = c
            self.prm_list.append((name, ncols))
            assert c + ncols <= NPRM
        return self.prm_cols[name]

    def pcol(self, name, ncols, j=0, rows=128):
        c = self.prm(name, ncols) + j
        return self.PRM.t[0:rows, c:c + 1]

    def pbc(self, name, n):
        c = self.prm(name, 8)
        return bass.AP(self.PRM.t, c, [[NPRM, 64], [1, 8], [0, n]])

    def sb(self, name, shape, dt, sem=False):
        t = self.nc.alloc_sbuf_tensor("sb_" + name, shape, dt)
        return Buf(name, t, self.S.new_sem("d_" + name) if sem else None)

    def rb(self):
        b = self.P[self.rot[self.rot_i % len(self.rot)]]
        self.rot_i += 1
        return b

    def wt(self, name, kc, mj):
        return self.W.tile((name, kc, mj))

    def mm(self, ob, oap, lb, lap, rb_, rap, start, stop, sig=None):
        self.S.op("pe", lambda e: e.matmul(oap, lhsT=lap, rhs=rap, start=start, stop=stop),
                  reads=[lb, rb_], writes=[ob], signal=stop if sig is None else sig)

    def tp(self, ob, oap, ib, iap, nrow):
        idb = self.identb
        self.S.op("pe", lambda e: e.transpose(oap, iap, idb.t[0:nrow, 0:nrow]), reads=[ib, idb], writes=[ob])

    def tt(self, eng, out, in0, in1, op, R, Wr):
        self.S.op(eng, lambda e: e.tensor_tensor(out=out, in0=in0, in1=in1, op=op), R, Wr)

    def stt(self, eng, out, in0, scalar, in1, op0, op1, R, Wr):
        self.S.op(eng, lambda e: e.scalar_tensor_tensor(out=out, in0=in0, scalar=scalar, in1=in1, op0=op0, op1=op1), R, Wr)

    def ts(self, eng, out, in0, s1, s2, op0, op1, R, Wr):
        if s2 is None:
            self.S.op(eng, lambda e: e.tensor_scalar(out=out, in0=in0, scalar1=s1, scalar2=None, op0=op0), R, Wr)
        else:
            self.S.op(eng, lambda e: e.tensor_scalar(out=out, in0=in0, scalar1=s1, scalar2=s2, op0=op0, op1=op1), R, Wr)

    def act(self, out, in_, func, R, Wr, **kw):
        self.S.op("act", lambda e: e.activation(out=out, in_=in_, func=func, **kw), R, Wr)

    def cp(self, eng, out, in_, R, Wr):
        if eng == "act":
            self.act(out, in_, AF.Copy, R, Wr)
        else:
            self.S.op(eng, lambda e: e.tensor_copy(out=out, in_=in_), R, Wr)

    def rcp(self, out, in_, R, Wr):
        self.S.op("dve", lambda e: e.reciprocal(out=out, in_=in_), R, Wr)

    def dma(self, q, out, in_, R, Wr, sb_):
        self.S.dma(q, lambda e: e.dma_start(out=out, in_=in_), R, Wr, sb_)

    def build(self):
        nc, S = self.nc, self.S
        stok = self.stok
        dt_in = lambda n, sh: nc.dram_tensor(n, sh, F32, kind="ExternalInput").ap()
        self.xT = dt_in("xT", [D, stok])
        self.outT = nc.dram_tensor("outT", [D, stok], F32, kind="ExternalOutput").ap()
        self.prm_d = dt_in("prm", [128, NPRM])
        self.rope_d = dt_in("rope", [4, 32, stok])
        self.ident_d = dt_in("ident", [128, 128])
        self.cmask_d = dt_in("cmask", [64, 3, 64])
        self.amask_d = dt_in("amask", [128, 2, T])
        self.lora_d = dt_in("lora", [128, 4, 8, 64])
        self.Kscr_d = nc.dram_tensor("Kscr", [8, 64, stok], BF16, kind="Internal").ap()
        self.Kpe_d = nc.dram_tensor("Kpescr", [32, stok], BF16, kind="Internal").ap()
        self.Vscr_d = nc.dram_tensor("Vscr", [8, 128, stok // 128, 65], BF16, kind="Internal").ap()
        self.Kscr = Buf("Kscr")
        self.Kpescr = Buf("Kpescr")
        self.Vscr = Buf("Vscr")
        self.W = WStream(S, nc)
        self.W.set_npass(self.nt)
        W = self.W
        S.bigval = lambda g: min(16 * CASTG * (g + 1), 16 * W.nblocks())

        self.P = [Buf(f"ps{i}", nc.alloc_psum_tensor(f"ps{i}", [128, 512], F32)) for i in range(8)]
        for b in self.P:
            b.tb = b.t.bitcast(BF16)
        sb = self.sb
        self.PRM = sb("PRM", [128, NPRM], F32, sem=True)
        self.ones32 = sb("ones32", [128, 128], F32)
        self.identb = sb("identb", [128, 128], BF16, sem=True)
        self.cmask = sb("cmask", [64, 3, 64], BF16, sem=True)
        self.amask = sb("amask", [128, 2, T], BF16, sem=True)
        self.lora = sb("lora", [128, 4, 8, 64], BF16, sem=True)
        self.rope = sb("rope", [128, 2, T], F32, sem=True)
        self.x32 = sb("x32", [128, NKC, T], F32, sem=True)
        self.xb = sb("xb", [128, NKC, T], BF16, sem=True)
        self.h1 = sb("h1", [128, NKC, T], F32, sem=True)
        self.h1b = sb("h1b", [128, NKC, T], BF16)
        self.g = sb("g", [128, NJ, T], BF16)
        self.tmp = [sb(f"tmp{i}", [128, T], F32) for i in range(6)]
        self.mean = sb("mean", [128, T], F32)
        self.rstd = sb("rstd", [128, T], F32)
        self.mixin = sb("mixin", [128, NKC, T], BF16)
        self.qlat = sb("qlat", [128, 3, T], F32)
        self.qn = sb("qn", [128, 3, T], BF16)
        self.kvlat = sb("kvlat", [128, 2, T], F32)
        self.kvn = sb("kvn", [128, 2, T], BF16)
        self.Qall = sb("Qall", [128, 8, T], BF16)
        self.Kst = sb("Kst", [64, 8, T], BF16, sem=True)
        self.kpe = sb("kpe", [32, T], BF16, sem=True)
        self.Vst = sb("Vst", [128, 8, T // 128, 65], BF16, sem=True)
        self.Kt = [sb(f"Kt{i}", [128, KSEG], BF16, sem=True) for i in range(2)]
        self.Vt = [sb(f"Vt{i}", [128, KSEG // 128, 65], BF16, sem=True) for i in range(2)]
        self.PT = [sb(f"PT{i}", [128, T], BF16) for i in range(3)]
        self.ymla = sb("ymla", [64, 8, T], BF16)
        self.rec = sb("rec", [128, T], F32)
        self.R32 = sb("R32", [64, 8, T], F32, sem=True)
        self.K32 = sb("K32", [64, 8, T], F32)
        self.V32 = sb("V32", [64, 8, T], F32)
        self.bigtmp = sb("bigtmp", [64, 8, T], F32, sem=True)
        self.LR = sb("LR", [64, 8, 1], F32)
        self.LK = sb("LK", [64, 8, 1], F32)
        self.LV = sb("LV", [64, 8, 1], F32)
        self.WD = sb("WD", [64, T], F32)
        self.AD = sb("AD", [64, T], F32)
        self.G1 = sb("G1", [128, T], F32)
        self.G2 = sb("G2", [32, T], F32)
        self.Lsm = sb("Lsm", [128, 4], F32)
        self.tw = sb("tw", [64, T], BF16)
        self.adb = sb("adb", [64, T], BF16)
        self.sg1 = sb("sg1", [128, T], BF16)
        self.sg2 = sb("sg2", [32, T], BF16)
        self.yrw = sb("yrw", [64, 8, T], BF16)
        self.cg = []
        for gi in range(NG):
            d = {}
            for n in ("SIG", "A32", "KKN", "B32", "KM", "BON", "E", "ct0", "ct1", "ct2", "S32", "tmpS", "Y32"):
                d[n] = sb(f"{n}_{gi}", [64, HG, CH], F32)
            for n in ("AT", "RT", "KT", "BT", "KH", "BH", "VB", "GATE", "Nm0", "Nm1", "Mm0", "Mm1", "Qm0", "Qm1",
                      "AKT", "RKT", "RBT", "Vtm", "KHtm", "BHtm", "P1s", "Us", "Sb0", "Sb1"):
                d[n] = sb(f"{n}_{gi}", [64, HG, CH], BF16)
            d["GC"] = sb(f"GC_{gi}", [64, HG], F32)
            self.cg.append(d)
        self.chunk_no = 0

        self.dma("pool", self.PRM.t[:], self.prm_d, [], [self.PRM], self.PRM)
        self.dma("pool", self.identb.t[:], self.ident_d, [], [self.identb], self.identb)
        self.dma("pool", self.cmask.t[:], self.cmask_d, [], [self.cmask], self.cmask)
        self.dma("pool", self.amask.t[:], self.amask_d, [], [self.amask], self.amask)
        self.dma("pool", self.lora.t[:], self.lora_d, [], [self.lora], self.lora)
        S.op("pool", lambda e: e.memset(self.ones32.t[:], 1.0), [], [self.ones32])
        for b in [self.LR, self.LK, self.LV, self.Lsm] + [d["S32"] for d in self.cg] + [d["Sb0"] for d in self.cg]:
            S.op("pool", lambda e, b=b: e.memset(b.t[:], 0.0), [], [b])
        S.op("pool", lambda e: e.memset(self.Vst.t[:], 1.0), [], [self.Vst])
        ka = self.prm("k_a", 8)
        om = self.prm("omka", 8)
        self.ts("dve", self.PRM.t[0:64, om:om + 8], self.PRM.t[0:64, ka:ka + 8], -1.0, 1.0, ALU.mult, ALU.add, [self.PRM], [self.PRM])

        def castfn(e):
            ins = None
            for b in range(W.nblocks()):
                if b % CASTG == 0 and b > 0:
                    e.wait_ge(W.castbuf.sem, 16 * b)
                ins = e.dma_start(out=W.wbf[b], in_=W.wf32[b])
                if b < W.nblocks() - 1:
                    ins.then_inc(W.castbuf.sem, 16)
            return ins
        S.streams["pool"].append(([], castfn, (W.castbuf.sem, 16)))
        W.castbuf.lw = (W.castbuf.sem, BIG + 10 ** 6)

        for it in range(self.nt):
            self.tile_body(it)
            W.end_pass()
        S.finish("pool", [self.h1, self.x32, self.R32, self.bigtmp])

        nb = W.nblocks()
        self.nb = nb
        W.wf32 = nc.dram_tensor("wf32", [nb, 128, WCOLS], F32, kind="ExternalInput").ap()
        W.wbf = nc.dram_tensor("wbf", [nb, 128, WCOLS], BF16, kind="Internal").ap()
        S.emit()

    def ln_stats(self, z, mj):
        S1, S2 = self.P[6], self.P[7]
        sq = self.tmp[mj % 2]
        self.act(sq.t[:], z.t[:, mj, :], AF.Square, [z], [sq])
        self.mm(S1, S1.t[:, 0:T], self.ones32, self.ones32.t[:], z, z.t[:, mj, :], mj == 0, mj == NKC - 1)
        self.mm(S2, S2.t[:, 0:T], self.ones32, self.ones32.t[:], sq, sq.t[:], mj == 0, mj == NKC - 1)

    def ln_apply(self, z, gname, bname, out32, outb, eps):
        S1, S2 = self.P[6], self.P[7]
        mean, rstd = self.mean, self.rstd
        self.act(mean.t[:], S1.t[:, 0:T], AF.Copy, [S1], [mean], scale=1.0 / D)
        self.tt("dve", rstd.t[:], mean.t[:], mean.t[:], ALU.mult, [mean], [rstd])
        self.stt("dve", rstd.t[:], S2.t[:, 0:T], 1.0 / D, rstd.t[:], ALU.mult, ALU.subtract, [S2, rstd], [rstd])
        self.rsqrt_(rstd, eps)
        for mj in range(NKC):
            tn = self.tmp[3 + mj % 2]
            self.tt("dve", tn.t[:], z.t[:, mj, :], mean.t[:], ALU.subtract, [z, mean], [tn])
            self.tt("dve", tn.t[:], tn.t[:], rstd.t[:], ALU.mult, [tn, rstd], [tn])
            self.act(out32.t[:, mj, :], tn.t[:], AF.Identity, [tn, self.PRM], [out32],
                     bias=self.pcol(bname, NKC, mj), scale=self.pcol(gname, NKC, mj))
            if outb is not None:
                self.act(outb.t[:, mj, :], tn.t[:], AF.Identity, [tn, self.PRM], [outb],
                         bias=self.pcol(bname, NKC, mj), scale=self.pcol(gname, NKC, mj))

    def ffn_ln(self, xin, xb, wn, gname, bname, out32, outb, after_w13=None):
        P = self.P
        for j in range(NJ):
            rows = 128 if j < NJ - 1 else DFF - 128 * (NJ - 1)
            A, B = P[(j % 2) * 2], P[(j % 2) * 2 + 1]
            for (bank, nm) in ((A, wn + "1"), (B, wn + "3")):
                for kc in range(NKC):
                    slot, off = self.wt(nm, kc, j)
                    self.mm(bank, bank.t[0:rows, 0:T], slot, slot.t[:, off:off + rows], xb, xb.t[:, kc, :], kc == 0, kc == NKC - 1)
            sl = self.tmp[j % 2]
            self.act(sl.t[0:rows, :], A.t[0:rows, 0:T], AF.Silu, [A], [sl])
            self.tt("dve", self.g.t[0:rows, j, :], sl.t[0:rows, :], B.t[0:rows, 0:T], ALU.mult, [B, sl], [self.g])
        if after_w13 is not None:
            after_w13()
        for mj in range(NKC):
            bank = P[4 + mj % 2]
            for kc in range(NJ):
                rows = 128 if kc < NJ - 1 else DFF - 128 * (NJ - 1)
                slot, off = self.wt(wn + "2", kc, mj)
                self.mm(bank, bank.t[:, 0:T], slot, slot.t[0:rows, off:off + 128], self.g, self.g.t[0:rows, kc, :], kc == 0, kc == NJ - 1)
            self.stt("dve", xin.t[:, mj, :], bank.t[:, 0:T], 0.5 / ALPHA, xin.t[:, mj, :], ALU.mult, ALU.add, [bank, xin], [xin])
            if mj > 0:
                self.ln_stats(xin, mj - 1)
        self.ln_stats(xin, NKC - 1)
        self.ln_apply(xin, gname, bname, out32, outb, LN_EPS / (ALPHA * ALPHA))

    def rms_proj(self, wname, nch, lat, outn, gname, dim):
        SS = self.P[7]
        for c in range(nch):
            bank = self.rb()
            for kc in range(NKC):
                slot, off = self.wt(wname, kc, c)
                self.mm(bank, bank.t[:, 0:T], slot, slot.t[:, off:off + 128], self.h1b, self.h1b.t[:, kc, :], kc == 0, kc == NKC - 1)
            self.cp("dve", lat.t[:, c, :], bank.t[:, 0:T], [bank], [lat])
            sq = self.tmp[c % 2]
            self.act(sq.t[:], lat.t[:, c, :], AF.Square, [lat], [sq])
            self.mm(SS, SS.t[:, 0:T], self.ones32, self.ones32.t[:], sq, sq.t[:], c == 0, c == nch - 1)
        t0, rs = self.tmp[2], self.tmp[3]
        self.ts("dve", rs.t[:], SS.t[:, 0:T], 1.0 / dim, 1e-6, ALU.mult, ALU.add, [SS], [rs])
        self.act(rs.t[:], rs.t[:], AF.Ln, [rs], [rs])
        self.act(rs.t[:], rs.t[:], AF.Exp, [rs], [rs], scale=-0.5)
        import os
        if os.environ.get("KSTT", "1") == "0":
            return
        for c in range(nch):
            self.stt("dve", outn.t[:, c, :], lat.t[:, c, :], self.pcol(gname, nch, c), rs.t[:], ALU.mult, ALU.mult, [lat, rs, self.PRM], [outn])

    def mla_tile(self, it):
        S, P = self.S, self.P
        t0 = it * T
        self.rot = [0, 1, 2, 3, 4, 5]
        self.dma("pool", self.rope.t[0:32, :, :], self.rope_d[0:2, :, t0:t0 + T].rearrange("a p t -> p a t"), [], [self.rope], self.rope)
        self.dma("pool", self.rope.t[64:96, :, :], self.rope_d[2:4, :, t0:t0 + T].rearrange("a p t -> p a t"), [], [self.rope], self.rope)
        import os
        if int(os.environ.get("KDBG", "9")) < 1:
            return
        self.rms_proj("win_q", 3, self.qlat, self.qn, "qn_g", 384.0)
        self.rms_proj("win_kv", 2, self.kvlat, self.kvn, "kvn_g", 256.0)
        import os
        if int(os.environ.get("KDBG", "9")) < 2:
            return
        bk = self.rb()
        for i, nm in enumerate(("win_kpe", "win_kpes")):
            for kc in range(NKC):
                slot, off = self.wt(nm, kc, 0)
                self.mm(bk, bk.t[0:32, i * T:(i + 1) * T], slot, slot.t[:, off:off + 32], self.h1b, self.h1b.t[:, kc, :], kc == 0, kc == NKC - 1)
        ta, tb_ = self.tmp[4], self.tmp[5]
        self.tt("dve", ta.t[0:32, :], bk.t[0:32, 0:T], self.rope.t[0:32, 0, :], ALU.mult, [bk, self.rope], [ta])
        self.tt("dve", tb_.t[0:32, :], bk.t[0:32, T:2 * T], self.rope.t[0:32, 1, :], ALU.mult, [bk, self.rope], [tb_])
        self.tt("pool", self.kpe.t[:], ta.t[0:32, :], tb_.t[0:32, :], ALU.add, [ta, tb_], [self.kpe])
        self.dma("pool", self.Kpe_d[:, t0:t0 + T], self.kpe.t[:], [self.kpe], [self.Kpescr], self.kpe)
        if int(os.environ.get("KDBG", "9")) < 3:
            return
        for h in range(8):
            bm, bs = self.rb(), self.rb()
            for (bank, nm) in ((bm, "wq_main"), (bs, "wq_swap")):
                for c in range(3):
                    slot, off = self.wt(nm, c, h)
                    self.mm(bank, bank.t[0:96, 0:T], slot, slot.t[:, off:off + 96], self.qn, self.qn.t[:, c, :], c == 0, c == 2)
            self.ts("dve", self.Qall.t[0:64, h, :], bm.t[0:64, 0:T], QSCALE, None, ALU.mult, None, [bm], [self.Qall])
            self.tt("dve", ta.t[64:96, :], bm.t[64:96, 0:T], self.rope.t[64:96, 0, :], ALU.mult, [bm, self.rope], [ta])
            self.tt("dve", tb_.t[64:96, :], bs.t[64:96, 0:T], self.rope.t[64:96, 1, :], ALU.mult, [bs, self.rope], [tb_])
            self.tt("pool", self.Qall.t[64:96, h, :], ta.t[64:96, :], tb_.t[64:96, :], ALU.add, [ta, tb_], [self.Qall])
        if int(os.environ.get("KDBG", "9")) < 4:
            return
        for h in range(8):
            bank = self.rb()
            for c in range(2):
                slot, off = self.wt("wkv_k", c, h)
                self.mm(bank, bank.t[0:64, 0:T], slot, slot.t[:, off:off + 64], self.kvn, self.kvn.t[:, c, :], c == 0, c == 1)
            self.cp("act", self.Kst.t[:, h, :], bank.t[0:64, 0:T], [bank], [self.Kst])
        self.dma("pool", self.Kscr_d[:, :, t0:t0 + T].rearrange("h r s -> r h s"), self.Kst.t[:], [self.Kst], [self.Kscr], self.Kst)
        for tb in range(T // 128):
            bank = self.rb()
            self.W.align(4)
            for c in range(2):
                sl0 = None
                for q in range(4):
                    slot, off = self.wt("wkv_v", c, q)
                    if q == 0:
                        sl0, off0 = slot, off
                    assert slot is sl0
                self.mm(bank, bank.t[:, :], self.kvn, self.kvn.t[:, c, tb * 128:(tb + 1) * 128], sl0, sl0.t[:, off0:off0 + 512], c == 0, c == 1)
            self.cp("act", self.Vst.t[:, :, tb, 0:64], bank.t[:, :].rearrange("p (h d) -> p h d", h=8), [bank], [self.Vst])
        self.dma("pool", self.Vscr_d[:, :, t0 // 128:t0 // 128 + T // 128, :].rearrange("h p k e -> p h k e"), self.Vst.t[:], [self.Vst], [self.Vscr], self.Vst)

    def attn_gen(self, it, every=1):
        P = self.P
        t0 = it * T
        nkb = (t0 + T) // 128
        SEGB = KSEG // 128
        nseg = (nkb + SEGB - 1) // SEGB
        items = [(h, sg) for h in range(8) for sg in range(nseg)]
        units = []
        for i, (h, sg) in enumerate(items):
            for kb in range(sg * SEGB, min(nkb, (sg + 1) * SEGB)):
                units.append((i, h, sg, kb))

        def load(i):
            h, sg = items[i]
            k0 = sg * KSEG
            nk = min(KSEG, t0 + T - k0)
            Kt, Vt = self.Kt[i % 2], self.Vt[i % 2]
            self.dma("pool", Kt.t[0:64, 0:nk], self.Kscr_d[h, :, k0:k0 + nk], [self.Kscr], [Kt], Kt)
            self.dma("pool", Kt.t[64:96, 0:nk], self.Kpe_d[:, k0:k0 + nk], [self.Kpescr], [Kt], Kt)
            self.dma("pool", Vt.t[:, 0:nk // 128, :], self.Vscr_d[h, :, k0 // 128:(k0 + nk) // 128, :], [self.Vscr], [Vt], Vt)

        cnt = [0]

        def bank():
            b = P[cnt[0] % 3]
            cnt[0] += 1
            return b

        def geom(u):
            i, h, sg, kb = u
            kl = kb - sg * SEGB
            loc = kb - (nkb - T // 128)
            q0 = max(loc, 0) * 128
            return kl, loc, q0

        scb = {}

        def emit_score(n):
            i, h, sg, kb = units[n]
            kl, loc, q0 = geom(units[n])
            Kt = self.Kt[i % 2]
            sc = bank()
            scb[n] = sc
            self.mm(sc, sc.t[:, q0:T], Kt, Kt.t[0:96, kl * 128:(kl + 1) * 128], self.Qall, self.Qall.t[0:96, h, q0:T], True, True)

        load(0)
        emit_score(0)
        emit_score(1)
        for n, u in enumerate(units):
            i, h, sg, kb = u
            kl, loc, q0 = geom(u)
            if kb == sg * SEGB and i + 1 < len(items):
                load(i + 1)
            if n + 2 < len(units):
                emit_score(n + 2)
            Vt = self.Vt[i % 2]
            O = P[3]
            sc = scb.pop(n)
            pt = self.PT[n % 3]
            self.act(pt.t[:, q0:T], sc.t[:, q0:T], AF.Exp, [sc], [pt])
            if loc >= 0:
                self.tt("pool", pt.t[:, q0:T], pt.t[:, q0:T], self.amask.t[:, loc, q0:T], ALU.mult, [pt, self.amask], [pt])
            self.mm(O, O.t[0:65, q0:T], Vt, Vt.t[:, kl, :], pt, pt.t[:, q0:T], kb == 0, kb == nkb - 1, sig=True)
            if kb == nkb - 1:
                rec, to, bc = self.rec, self.tmp[h % 2], sc
                self.cp("act", to.t[0:65, :], O.t[0:65, 0:T], [O], [to])
                self.rcp(rec.t[64:65, :], to.t[64:65, :], [to], [rec])
                self.mm(bc, bc.t[0:64, 0:T], self.ones32, self.ones32.t[64:65, 0:64], rec, rec.t[64:65, :], True, True)
                self.tt("dve", self.ymla.t[:, h, :], to.t[0:64, :], bc.t[0:64, 0:T], ALU.mult, [to, bc], [self.ymla])
            if (n + 1) % every == 0:
                yield

    def shift3(self, X, L, muname):
        d = self.bigtmp
        self.tt("dve", d.t[:, :, 1:T], X.t[:, :, 0:T - 1], X.t[:, :, 1:T], ALU.subtract, [X], [d])
        self.tt("pool", d.t[:, :, 0:1], L.t[:], X.t[:, :, 0:1], ALU.subtract, [X, L], [d])
        self.cp("pool", L.t[:], X.t[:, :, T - 1:T], [X], [L])
        self.tt("dve", d.t[:], d.t[:], self.pbc(muname, T), ALU.mult, [d, self.PRM], [d])
        self.tt("dve", X.t[:], X.t[:], d.t[:], ALU.add, [X, d], [X])

    def shift2(self, X, rows, lcol, muname):
        d = self.tmp[2]
        L = self.Lsm
        self.tt("dve", d.t[0:rows, 1:T], X.t[0:rows, 0:T - 1], X.t[0:rows, 1:T], ALU.subtract, [X], [d])
        self.tt("pool", d.t[0:rows, 0:1], L.t[0:rows, lcol:lcol + 1], X.t[0:rows, 0:1], ALU.subtract, [X, L], [d])
        self.cp("pool", L.t[0:rows, lcol:lcol + 1], X.t[0:rows, T - 1:T], [X], [L])
        self.stt("dve", X.t[0:rows, :], d.t[0:rows, :], self.pcol(muname, 1, 0, rows), X.t[0:rows, :], ALU.mult, ALU.add, [d, X, self.PRM], [X])

    def rwkv_tile(self, it):
        P = self.P
        self.rot = list(range(8))
        h1b = self.h1b
        for nm, X in (("win_r", self.R32), ("win_k", self.K32), ("win_v", self.V32)):
            for hp in range(4):
                bank = self.rb()
                for hh in range(2):
                    h = hp * 2 + hh
                    for kc in range(NKC):
                        slot, off = self.wt(nm, kc, h)
                        self.mm(bank, bank.t[0:64, hh * T:(hh + 1) * T], slot, slot.t[:, off:off + 64], h1b, h1b.t[:, kc, :], kc == 0, kc == NKC - 1)
                self.cp("act", X.t[:, hp * 2:hp * 2 + 2, :], bank.t[0:64, :].rearrange("p (h t) -> p h t", h=2), [bank], [X])
        bA, bB = self.rb(), self.rb()
        for (bank, lo, rows, nm, mj, X) in ((bA, 0, 64, "win_wd", 0, self.WD), (bA, T, 64, "win_ad", 0, self.AD),
                                            (bB, 0, 128, "win_gd", 0, self.G1), (bB, T, 32, "win_gd", 1, self.G2)):
            for kc in range(NKC):
                slot, off = self.wt(nm, kc, mj)
                self.mm(bank, bank.t[0:rows, lo:lo + T], slot, slot.t[:, off:off + rows], h1b, h1b.t[:, kc, :], kc == 0, kc == NKC - 1)
            self.cp("act", X.t[0:rows, :], bank.t[0:rows, lo:lo + T], [bank], [X])
        self.shift3(self.R32, self.LR, "mu_r")
        self.shift3(self.K32, self.LK, "mu_k")
        self.shift3(self.V32, self.LV, "mu_v")
        self.shift2(self.WD, 64, 0, "mu_wd")
        self.shift2(self.AD, 64, 1, "mu_ad")
        self.shift2(self.G1, 128, 2, "mu_g1")
        self.shift2(self.G2, 32, 3, "mu_g2")
        self.act(self.tw.t[:], self.WD.t[:], AF.Tanh, [self.WD], [self.tw])
        self.cp("pool", self.adb.t[:], self.AD.t[:], [self.AD], [self.adb])
        self.act(self.sg1.t[:], self.G1.t[:], AF.Sigmoid, [self.G1], [self.sg1])
        self.act(self.sg2.t[:], self.G2.t[:], AF.Sigmoid, [self.G2], [self.sg2])

    def rwkv_chunks_gen(self):
        for ci in range(T // CH):
            yield from inter([self.rwkv_chunk(ci, gi) for gi in range(NG)])
        self.chunk_no += T // CH

    def pbg(self, name, gi, n):
        c = self.prm(name, 8) + gi * HG
        return bass.AP(self.PRM.t, c, [[NPRM, 64], [1, HG], [0, n]])

    def rbg(self, gi):
        nb = 4 // NG
        b = self.P[4 + gi * nb + self.rotg[gi] % nb]
        self.rotg[gi] += 1
        return b

    def rsqrt_(self, buf, eps):
        self.ts("dve", buf.t[:], buf.t[:], 0.0, eps, ALU.max, ALU.add, [buf], [buf])
        self.act(buf.t[:], buf.t[:], AF.Ln, [buf], [buf])
        self.act(buf.t[:], buf.t[:], AF.Exp, [buf], [buf], scale=-0.5)

    def rwkv_chunk(self, ci, gi):
        cs = slice(ci * CH, (ci + 1) * CH)
        h0 = gi * HG
        G = self.cg[gi]
        Rc, Kc, Vc = self.R32.t[:, h0:h0 + HG, cs], self.K32.t[:, h0:h0 + HG, cs], self.V32.t[:, h0:h0 + HG, cs]
        RR, KK, VV = [self.R32], [self.K32], [self.V32]
        PR = self.PRM
        lora = self.lora
        SIG, A32, KKN, B32, KM, BON, E = [G[n] for n in ("SIG", "A32", "KKN", "B32", "KM", "BON", "E")]
        ct0, ct1, ct2 = G["ct0"], G["ct1"], G["ct2"]
        AT, RT, KT, BT, KH, BH, VB, GATE = [G[n] for n in ("AT", "RT", "KT", "BT", "KH", "BH", "VB", "GATE")]
        W_ = HG * CH
        v8 = lambda bank: bank.t[0:64, 0:W_].rearrange("p (h t) -> p h t", h=HG)
        v8b = lambda bank: bank.tb[0:64, 0:W_].rearrange("p (h t) -> p h t", h=HG)
        flat = lambda b: b.t[:].rearrange("p h t -> p (h t)")
        ones64 = self.ones32.t[0:64, 0:64]
        rb = lambda: self.rbg(gi)
        bc = lambda name: self.pbg(name, gi, CH)
        b1 = rb()
        for h in range(HG):
            self.mm(b1, b1.t[0:64, h * CH:(h + 1) * CH], lora, lora.t[0:64, 0, h0 + h, :], self.tw, self.tw.t[:, cs], True, True)
        yield
        self.tt("dve", SIG.t[:], v8(b1), bc("w0"), ALU.add, [b1, PR], [SIG])
        yield
        self.act(SIG.t[:], SIG.t[:], AF.Exp, [SIG], [SIG], scale=-1.0)
        self.act(SIG.t[:], SIG.t[:], AF.Ln, [SIG], [SIG], bias=1.0)
        self.act(SIG.t[:], SIG.t[:], AF.Exp, [SIG], [SIG], scale=-1.0)
        yield
        b2 = rb()
        for h in range(HG):
            self.mm(b2, b2.t[0:64, h * CH:(h + 1) * CH], lora, lora.t[0:64, 1, h0 + h, :], self.adb, self.adb.t[:, cs], True, True)
        yield
        self.tt("dve", A32.t[:], v8(b2), bc("a0"), ALU.add, [b2, PR], [A32])
        yield
        self.act(A32.t[:], A32.t[:], AF.Exp, [A32], [A32], scale=-1.0)
        self.act(A32.t[:], A32.t[:], AF.Ln, [A32], [A32], bias=1.0)
        self.act(A32.t[:], A32.t[:], AF.Exp, [A32], [A32], scale=-1.0)
        yield
        b3 = rb()
        for h in range(HG):
            self.mm(b3, b3.t[0:64, h * CH:(h + 1) * CH], lora, lora.t[:, 2, h0 + h, :], self.sg1, self.sg1.t[:, cs], True, False)
            self.mm(b3, b3.t[0:64, h * CH:(h + 1) * CH], lora, lora.t[0:32, 3, h0 + h, :], self.sg2, self.sg2.t[0:32, cs], False, True)
        yield
        self.cp("act", GATE.t[:], v8(b3), [b3], [GATE])
        yield
        self.tt("dve", KKN.t[:], Kc, bc("k_k"), ALU.mult, KK + [PR], [KKN])
        yield
        self.act(ct0.t[:], KKN.t[:], AF.Square, [KKN], [ct0])
        yield
        b4 = rb()
        self.mm(b4, b4.t[0:64, 0:W_], self.ones32, ones64, ct0, flat(ct0), True, True)
        yield
        self.cp("dve", ct1.t[:], v8(b4), [b4], [ct1])
        self.rsqrt_(ct1, 1e-24)
        yield
        self.tt("dve", KKN.t[:], KKN.t[:], ct1.t[:], ALU.mult, [KKN, ct1], [KKN])
        yield
        self.tt("dve", ct0.t[:], A32.t[:], bc("k_a"), ALU.mult, [A32, PR], [ct0])
        yield
        self.tt("dve", ct0.t[:], ct0.t[:], bc("omka"), ALU.add, [ct0, PR], [ct0])
        yield
        self.tt("dve", KM.t[:], Kc, ct0.t[:], ALU.mult, KK + [ct0], [KM])
        yield
        self.tt("pool", B32.t[:], KKN.t[:], A32.t[:], ALU.mult, [KKN, A32], [B32])
        yield
        self.tt("dve", ct0.t[:], Rc, KM.t[:], ALU.mult, RR + [KM], [ct0])
        yield
        self.tt("dve", ct0.t[:], ct0.t[:], bc("r_k"), ALU.mult, [ct0, PR], [ct0])
        yield
        b5 = rb()
        self.mm(b5, b5.t[0:64, 0:W_], self.ones32, ones64, ct0, flat(ct0), True, True)
        yield
        self.tt("dve", BON.t[:], v8(b5), Vc, ALU.mult, [b5] + VV, [BON])
        yield
        src, dst = SIG, ct2
        for d in (1, 2, 4, 8, 16, 32):
            self.tt("dve", dst.t[:, :, d:CH], src.t[:, :, d:CH], src.t[:, :, 0:CH - d], ALU.add, [src], [dst])
            self.cp("pool", dst.t[:, :, 0:d], src.t[:, :, 0:d], [src], [dst])
            src, dst = dst, src
            yield
        assert src is SIG
        self.act(E.t[:], SIG.t[:], AF.Exp, [SIG], [E], scale=-C0)
        yield
        self.tt("dve", RT.t[:], Rc, E.t[:], ALU.mult, RR + [E], [RT])
        yield
        self.stt("dve", AT.t[:, :, 1:CH], KKN.t[:, :, 1:CH], -1.0, E.t[:, :, 0:CH - 1], ALU.mult, ALU.mult, [KKN, E], [AT])
        yield
        self.ts("pool", AT.t[:, :, 0:1], KKN.t[:, :, 0:1], -1.0, None, ALU.mult, None, [KKN], [AT])
        self.cp("pool", G["GC"].t[:, :], E.t[:, :, CH - 1], [E], [G["GC"]])
        yield
        self.act(E.t[:], SIG.t[:], AF.Exp, [SIG], [E], scale=C0)
        yield
        self.tt("dve", KT.t[:], KM.t[:], E.t[:], ALU.mult, [KM, E], [KT])
        yield
        self.tt("pool", BT.t[:], B32.t[:], E.t[:], ALU.mult, [B32, E], [BT])
        yield
        last_bc = bass.AP(SIG.t, CH - 1, [[HG * CH, 64], [CH, HG], [0, CH]])
        self.tt("dve", ct0.t[:], last_bc, SIG.t[:], ALU.subtract, [SIG], [ct0])
        yield
        self.act(E.t[:], ct0.t[:], AF.Exp, [ct0], [E], scale=-C0)
        yield
        self.tt("dve", KH.t[:], KM.t[:], E.t[:], ALU.mult, [KM, E], [KH])
        yield
        self.tt("pool", BH.t[:], B32.t[:], E.t[:], ALU.mult, [B32, E], [BH])
        yield
        self.cp("act", VB.t[:], Vc, VV, [VB])
        yield
        cm = self.cmask
        mbc = lambda i: bass.AP(cm.t, i * 64, [[3 * 64, 64], [0, HG], [1, 64]])
        N0, M0, Q0 = G["Nm0"], G["Mm0"], G["Qm0"]
        for (L, R, dstb, mi) in ((AT, BT, N0, 0), (BT, AT, M0, 1), (KT, AT, G["AKT"], 1),
                                 (KT, RT, G["RKT"], 2), (BT, RT, G["RBT"], 2)):
            bank = rb()
            for h in range(HG):
                self.mm(bank, bank.t[0:64, h * CH:(h + 1) * CH], L, L.t[:, h, :], R, R.t[:, h, :], True, True)
            yield
            self.tt("dve", dstb.t[:], v8(bank), mbc(mi), ALU.mult, [bank, cm], [dstb])
            yield
        idbc = bass.AP(self.identb.t, 0, [[128, 64], [0, HG], [1, 64]])
        self.tt("dve", Q0.t[:], M0.t[:], idbc, ALU.add, [M0, self.identb], [Q0])
        yield
        cur = 0
        pend = None
        for j in range(1, 6):
            No, Mo = G[f"Nm{cur}"], G[f"Mm{cur}"]
            Nn, Mn = G[f"Nm{1 - cur}"], G[f"Mm{1 - cur}"]
            bn = rb()
            for h in range(HG):
                self.mm(bn, bn.t[0:64, h * CH:(h + 1) * CH], Mo, Mo.t[:, h, :], No, No.t[:, h, :], True, True)
            yield
            self.cp("act", Nn.t[:], v8(bn), [bn], [Nn])
            yield
            if j < 5:
                bm_ = rb()
                for h in range(HG):
                    self.mm(bm_, bm_.t[0:64, h * CH:(h + 1) * CH], No, No.t[:, h, :], Mo, Mo.t[:, h, :], True, True)
                yield
                self.cp("dve", Mn.t[:], v8(bm_), [bm_], [Mn])
                yield
            Qo, Qn = G[f"Qm{(j - 1) % 2}"], G[f"Qm{j % 2}"]
            bq = rb()
            for h in range(HG):
                self.mm(bq, bq.t[0:64, h * CH:(h + 1) * CH], Nn, Nn.t[:, h, :], Qo, Qo.t[:, h, :], True, True)
            yield
            self.tt("dve", Qn.t[:], v8(bq), Qo.t[:], ALU.add, [bq, Qo], [Qn])
            yield
            cur = 1 - cur
        cur = 5 % 2
        Qf = G[f"Qm{cur}"]
        for (X, Xtm) in ((VB, G["Vtm"]), (KH, G["KHtm"]), (BH, G["BHtm"])):
            bank = rb()
            for h in range(HG):
                self.tp(bank, bank.tb[0:64, h * CH:(h + 1) * CH], X, X.t[:, h, :], 64)
            yield
            self.cp("act", Xtm.t[:], v8b(bank), [bank], [Xtm])
            yield
        cn = self.chunk_no + ci
        Sb_cur, Sb_nxt = G[f"Sb{cn % 2}"], G[f"Sb{(cn + 1) % 2}"]
        S32, tmpS, Y, P1s, Us, Vtm = G["S32"], G["tmpS"], G["Y32"], G["P1s"], G["Us"], G["Vtm"]
        bp = rb()
        for h in range(HG):
            o = bp.t[0:64, h * CH:(h + 1) * CH]
            self.mm(bp, o, G["AKT"], G["AKT"].t[:, h, :], Vtm, Vtm.t[:, h, :], True, False)
            self.mm(bp, o, AT, AT.t[:, h, :], Sb_cur, Sb_cur.t[:, h, :], False, True)
        yield
        self.cp("act", P1s.t[:], v8(bp), [bp], [P1s])
        yield
        bu = rb()
        for h in range(HG):
            self.mm(bu, bu.t[0:64, h * CH:(h + 1) * CH], Qf, Qf.t[:, h, :], P1s, P1s.t[:, h, :], True, True)
        yield
        self.cp("dve", Us.t[:], v8(bu), [bu], [Us])
        yield
        bd = rb()
        for h in range(HG):
            o = bd.t[0:64, h * CH:(h + 1) * CH]
            self.mm(bd, o, G["KHtm"], G["KHtm"].t[:, h, :], Vtm, Vtm.t[:, h, :], True, False)
            self.mm(bd, o, G["BHtm"], G["BHtm"].t[:, h, :], Us, Us.t[:, h, :], False, True)
        by = rb()
        for h in range(HG):
            o = by.t[0:64, h * CH:(h + 1) * CH]
            self.mm(by, o, Sb_cur, Sb_cur.t[:, h, :], RT, RT.t[:, h, :], True, False)
            self.mm(by, o, Vtm, Vtm.t[:, h, :], G["RKT"], G["RKT"].t[:, h, :], False, False)
            self.mm(by, o, Us, Us.t[:, h, :], G["RBT"], G["RBT"].t[:, h, :], False, True)
        yield
        gcb = bass.AP(G["GC"].t, 0, [[HG, 64], [1, HG], [0, CH]])
        self.tt("pool", tmpS.t[:], S32.t[:], gcb, ALU.mult, [S32, G["GC"]], [tmpS])
        yield
        self.tt("dve", S32.t[:], tmpS.t[:], v8(bd), ALU.add, [tmpS, bd], [S32])
        yield
        self.cp("act", Sb_nxt.t[:], S32.t[:], [S32], [Sb_nxt])
        self.cp("act", Y.t[:], v8(by), [by], [Y])
        yield
        self.act(ct0.t[:], Y.t[:], AF.Square, [Y], [ct0])
        s1, s2 = rb(), rb()
        self.mm(s1, s1.t[0:64, 0:W_], self.ones32, ones64, Y, flat(Y), True, True)
        yield
        self.mm(s2, s2.t[0:64, 0:W_], self.ones32, ones64, ct0, flat(ct0), True, True)
        self.act(ct1.t[:], v8(s1), AF.Copy, [s1], [ct1], scale=1.0 / 64)
        yield
        self.tt("dve", ct2.t[:], ct1.t[:], ct1.t[:], ALU.mult, [ct1], [ct2])
        yield
        self.stt("dve", ct2.t[:], v8(s2), 1.0 / 64, ct2.t[:], ALU.mult, ALU.subtract, [s2, ct2], [ct2])
        self.rsqrt_(ct2, GN_EPS)
        yield
        self.tt("dve", ct0.t[:], Y.t[:], ct1.t[:], ALU.subtract, [Y, ct1], [ct0])
        yield
        self.tt("dve", ct0.t[:], ct0.t[:], ct2.t[:], ALU.mult, [ct0, ct2], [ct0])
        yield
        self.tt("dve", ct0.t[:], ct0.t[:], bc("gn_g"), ALU.mult, [ct0, PR], [ct0])
        yield
        self.tt("dve", ct0.t[:], ct0.t[:], bc("gn_b"), ALU.add, [ct0, PR], [ct0])
        yield
        self.tt("dve", ct0.t[:], ct0.t[:], BON.t[:], ALU.add, [ct0, BON], [ct0])
        yield
        self.tt("dve", self.yrw.t[:, h0:h0 + HG, cs], ct0.t[:], GATE.t[:], ALU.mult, [ct0, GATE], [self.yrw])
        yield

    def mix_tile(self):
        P = self.P
        h1, h1b = self.h1, self.h1b
        for mj in range(NKC):
            parts = []
            for (yb, wn, gn, tmpi) in ((self.yrw, "w_up_rwkv", "win_gr", 0), (self.ymla, "w_up_mla", "win_gm", 1)):
                by = P[tmpi * 2 + 4 * (mj % 2)]
                bg = P[tmpi * 2 + 1 + 4 * (mj % 2)]
                for h in range(8):
                    slot, off = self.wt(wn, h, mj)
                    self.mm(by, by.t[:, 0:T], slot, slot.t[0:64, off:off + 128], yb, yb.t[:, h, :], h == 0, h == 7)
                for kc in range(NKC):
                    slot, off = self.wt(gn, kc, mj)
                    self.mm(bg, bg.t[:, 0:T], slot, slot.t[:, off:off + 128], h1b, h1b.t[:, kc, :], kc == 0, kc == NKC - 1)
                sg = self.tmp[tmpi + 2 * (mj % 2)]
                self.act(sg.t[:], bg.t[:, 0:T], AF.Sigmoid, [bg], [sg])
                self.tt("dve", sg.t[:], sg.t[:], by.t[:, 0:T], ALU.mult, [sg, by], [sg])
                parts.append(sg)
            self.tt("pool", self.mixin.t[:, mj, :], parts[0].t[:], parts[1].t[:], ALU.add, parts, [self.mixin])
        for mo in range(NKC):
            bank = P[mo % 4]
            for mj in range(NKC):
                slot, off = self.wt("w_o", mj, mo)
                self.mm(bank, bank.t[:, 0:T], slot, slot.t[:, off:off + 128], self.mixin, self.mixin.t[:, mj, :], mj == 0, mj == NKC - 1)
            self.stt("dve", h1.t[:, mo, :], bank.t[:, 0:T], 1.0 / ALPHA, h1.t[:, mo, :], ALU.mult, ALU.add, [bank, h1], [h1])
            if mo > 0:
                self.ln_stats(h1, mo - 1)
        self.ln_stats(h1, NKC - 1)
        self.ln_apply(h1, "ln2_g", "ln2_b", self.x32, self.xb, LN_EPS / (ALPHA * ALPHA))

    def store(self, src, nrow_chunks, t0, rows=128):
        dst = self.outT.rearrange("(kc p) s -> p kc s", p=128)[0:rows, 0:nrow_chunks, t0:t0 + T]
        self.dma("pool", dst, src.t[:], [src], [], src)

    def load_xb(self, it):
        src = self.xT.rearrange("(kc p) s -> p kc s", p=128)[:, :, it * T:(it + 1) * T]
        self.dma("pool", self.xb.t[:], src, [], [self.xb], self.xb)

    def tile_body(self, it):
        S = self.S
        t0 = it * T
        x32 = self.x32
        xsrc = self.xT.rearrange("(kc p) s -> p kc s", p=128)[:, :, t0:t0 + T]
        self.dma("sp", x32.t[:], xsrc, [], [x32], x32)
        if it == 0:
            self.load_xb(0)
        self.ffn_ln(x32, self.xb, "ffn1_w", "ln1_g", "ln1_b", self.h1, self.h1b)
        if self.stage == "ffn1":
            return self.store(self.h1, NKC, t0)
        if self.stage in ("mla", "full", "h2"):
            self.mla_tile(it)
            if self.stage == "mla":
                interleave([self.attn_gen(it)])
        if self.stage == "mla":
            dst = self.outT[0:512, t0:t0 + T].rearrange("(h p) s -> p h s", p=64)
            self.S.op("act", lambda e: e.activation(out=self.R32.t[:], in_=self.ymla.t[:], func=AF.Copy), [self.ymla], [self.R32])
            self.dma("sp", dst, self.R32.t[:], [self.R32], [], self.R32)
            return
        if self.stage in ("rwkv", "full", "h2"):
            self.rwkv_tile(it)
            if self.stage == "rwkv":
                interleave([self.rwkv_chunks_gen()])
            else:
                nA = 8 * ((t0 + T) // 128)
                nB = 75 * (T // CH)
                interleave([self.attn_gen(it, every=max(1, round(nA / nB))), self.rwkv_chunks_gen()])
        if self.stage == "rwkv":
            dst = self.outT[0:512, t0:t0 + T].rearrange("(h p) s -> p h s", p=64)
            self.S.op("act", lambda e: e.activation(out=self.bigtmp.t[:], in_=self.yrw.t[:], func=AF.Copy), [self.yrw], [self.bigtmp])
            self.dma("sp", dst, self.bigtmp.t[:], [self.bigtmp], [], self.bigtmp)
            return
        self.mix_tile()
        if self.stage == "h2":
            return self.store(self.x32, NKC, t0)
        self.ffn_ln(self.x32, self.xb, "ffn2_w", "ln3_g", "ln3_b", self.h1, None,
                    after_w13=(lambda: self.load_xb(it + 1)) if it + 1 < self.nt else None)
        self.store(self.h1, NKC, t0)


_PROG_CACHE = {}


def get_prog(stok, stage):
    k = (stok, stage)
    if k not in _PROG_CACHE:
        _PROG_CACHE[k] = Prog(stok, stage)
    return _PROG_CACHE[k]


def pack_weights(prog, mats):
    order = prog.W.order
    nb = prog.nb
    wf = np.zeros((nb, 128, WCOLS), np.float32)
    for pos, key in enumerate(order):
        if key is None:
            continue
        name, kc, mj = key
        blk = mats[name][kc * 128:(kc + 1) * 128, mj * 128:(mj + 1) * 128]
        b, o = pos // TPB, (pos % TPB) * 128
        wf[b, :blk.shape[0], o:o + blk.shape[1]] = blk
    return wf


def fm(v):
    return np.ascontiguousarray(np.asarray(v, np.float32).reshape(-1, 128).T)


def h8(v):
    return np.ascontiguousarray(np.asarray(v, np.float32).reshape(8, 64).T)


def headpad_cols(w, per):
    K = w.shape[0]
    out = np.zeros((K, 8 * 128), np.float32)
    for h in range(8):
        out[:, h * 128:h * 128 + per] = w[:, h * per:(h + 1) * per]
    return out


def headpad_rows(w, per):
    M = w.shape[1]
    out = np.zeros((8 * 128, M), np.float32)
    for h in range(8):
        out[h * 128:h * 128 + per, :] = w[h * per:(h + 1) * per, :]
    return out


def host_constants(stok):
    inv = (10000.0 ** (-np.arange(0, 32, 2, dtype=np.float32) / 32)).astype(np.float32)
    ang = np.arange(stok, dtype=np.float32)[None, :] * inv[:, None]
    cos, sin = np.cos(ang), np.sin(ang)
    cosF = np.concatenate([cos, cos], 0)
    sinS = np.concatenate([-sin, sin], 0)
    rope = np.stack([cosF, sinS, cosF * QSCALE, sinS * QSCALE], 0).astype(np.float32)
    ident = np.eye(128, dtype=np.float32)
    i = np.arange(64)
    cmask = np.zeros((64, 3, 64), np.float32)
    cmask[:, 0, :] = (i[None, :] < i[:, None])
    cmask[:, 1, :] = (i[:, None] < i[None, :])
    cmask[:, 2, :] = (i[:, None] <= i[None, :])
    amask = np.zeros((128, T // 128, T), np.float32)
    p = np.arange(128)
    q = np.arange(T)
    for kl in range(T // 128):
        amask[:, kl, :] = ((kl * 128 + p)[:, None] // 64) <= (q[None, :] // 64)
    return rope, ident, cmask, amask


def run(inputs, stok=None, stage="full", ncores=8):
    x = np.asarray(inputs["x"], np.float32)
    if stok is None:
        stok = x.shape[1]
    prog = get_prog(stok, stage)
    L = 0
    g = lambda n: np.asarray(inputs[n][L], np.float32)
    w_in = g("w_in")
    c = 0
    cols = {}
    for nm, n in (("r", 512), ("k", 512), ("v", 512), ("wd", 64), ("ad", 64), ("gd", 160), ("q", 384), ("kv", 256), ("kpe", 32), ("gr", 1024), ("gm", 1024)):
        cols[nm] = w_in[:, c:c + n]
        c += n
    wq = g("w_q_up").reshape(384, 8, 96)
    wq_main = wq.reshape(384, 768)
    wq_swap = np.concatenate([wq[:, :, 0:64], wq[:, :, 80:96], wq[:, :, 64:80]], axis=2).reshape(384, 768)
    wkv = g("w_kv_up").reshape(256, 8, 128)
    mats = {
        "ffn1_w1": g("ffn1_w1"), "ffn1_w3": g("ffn1_w3"), "ffn1_w2": g("ffn1_w2"),
        "ffn2_w1": g("ffn2_w1"), "ffn2_w3": g("ffn2_w3"), "ffn2_w2": g("ffn2_w2"),
        "win_q": cols["q"], "win_kv": cols["kv"], "win_kpe": cols["kpe"],
        "win_kpes": np.concatenate([cols["kpe"][:, 16:32], cols["kpe"][:, 0:16]], 1),
        "wq_main": headpad_cols(wq_main, 96), "wq_swap": headpad_cols(wq_swap, 96),
        "wkv_k": headpad_cols(np.ascontiguousarray(wkv[:, :, 0:64]).reshape(256, 512), 64),
        "wkv_v": np.ascontiguousarray(wkv[:, :, 64:128]).reshape(256, 512),
        "win_r": headpad_cols(cols["r"], 64), "win_k": headpad_cols(cols["k"], 64), "win_v": headpad_cols(cols["v"], 64),
        "win_wd": cols["wd"], "win_ad": cols["ad"], "win_gd": cols["gd"],
        "win_gr": cols["gr"], "win_gm": cols["gm"],
        "w_up_rwkv": headpad_rows(g("w_up_rwkv"), 64), "w_up_mla": headpad_rows(g("w_up_mla"), 64),
        "w_o": g("w_o"),
    }
    mu = g("mu_shift")
    vecs = {
        "ln1_g": fm(g("ln1_g")), "ln1_b": fm(g("ln1_b")), "ln2_g": fm(g("ln2_g")), "ln2_b": fm(g("ln2_b")),
        "ln3_g": fm(g("ln3_g")), "ln3_b": fm(g("ln3_b")),
        "qn_g": fm(g("q_norm_g")), "kvn_g": fm(g("kv_norm_g")),
        "mu_r": h8(mu[0:512]), "mu_k": h8(mu[512:1024]), "mu_v": h8(mu[1024:1536]),
        "mu_wd": mu[1536:1600, None], "mu_ad": mu[1600:1664, None], "mu_g1": mu[1664:1792, None], "mu_g2": mu[1792:1824, None],
        "w0": h8(g("w0")), "a0": h8(g("a0")), "k_k": h8(g("k_k")), "k_a": h8(g("k_a")), "r_k": h8(g("r_k").reshape(-1)),
        "gn_g": h8(g("gn_g")), "gn_b": h8(g("gn_b")),
    }
    prm = np.zeros((128, NPRM), np.float32)
    for name, n in prog.prm_list:
        if name in vecs:
            v = vecs[name]
            prm[:v.shape[0], prog.prm_cols[name]:prog.prm_cols[name] + n] = v
    lora = np.zeros((128, 4, 8, 64), np.float32)
    lora[0:64, 0] = g("w_decay_up").reshape(64, 8, 64)
    lora[0:64, 1] = g("w_iclr_up").reshape(64, 8, 64)
    wg = g("w_gate_up").reshape(160, 8, 64)
    lora[:, 2] = wg[0:128]
    lora[0:32, 3] = wg[128:160]
    wf = pack_weights(prog, mats)
    rope, ident, cmask, amask = host_constants(stok)
    in_maps = []
    for c in range(ncores):
        xT = np.ascontiguousarray(x[c, :stok, :].T)
        in_maps.append({"xT": xT, "prm": prm, "wf32": wf, "rope": rope, "ident": ident, "cmask": cmask, "amask": amask, "lora": lora})
    res = run_bass_kernel_spmd(prog.nc, in_maps, core_ids=list(range(ncores)))
    out = np.stack([np.ascontiguousarray(res.results[c]["outT"].T) for c in range(ncores)], axis=0)
    return out


def kernel(**inputs):
    return run(inputs).astype(np.float32)
```

```python
import contextlib
import numpy as np
import ml_dtypes
import concourse.bass as bass
import concourse.mybir as mybir
from concourse.bass_utils import run_bass_kernel_spmd

F32 = mybir.dt.float32
BF16 = mybir.dt.bfloat16
AF = mybir.ActivationFunctionType
ALU = mybir.AluOpType

D = 1024
DFF = 2752
T = 256
NKC = D // 128
NJ = (DFF + 127) // 128
ALPHA = 2.0 ** 0.25
LN_EPS = 1e-5
WCOLS = 2048
TPB = WCOLS // 128
NSLOT = 6
PREFETCH = 4
NPRM = 160
KSEG = 1024
CH = 64
QSCALE = 96.0 ** -0.5
C0 = float(np.exp(-0.5))
GN_EPS = 64e-5
NG = 2
HG = 8 // NG


def inter(gens):
    gens = list(gens)
    while gens:
        for g_ in list(gens):
            try:
                next(g_)
            except StopIteration:
                gens.remove(g_)
        yield


def interleave(gens):
    for _ in inter(gens):
        pass


ENGS = ("pe", "act", "dve", "pool", "sp")
BIG = 1 << 30


class Buf:
    def __init__(self, name, t=None, sem=None):
        self.name = name
        self.t = t
        self.sem = sem
        self.semval = 0
        self.lw = None
        self.rd = {}


class Sched:
    def __init__(self, nc, stack):
        self.nc = nc
        self.stack = stack
        self.streams = {e: [] for e in ENGS}
        self.esem = {}
        for e in ("pe", "act", "dve", "pool"):
            self.esem[e] = stack.enter_context(nc.semaphore("es_" + e))
        self.ecnt = {e: 0 for e in ENGS}
        self.waited = {e: {} for e in ENGS}
        self.nsem = 4
        self.bigval = None

    def new_sem(self, name):
        self.nsem += 1
        return self.stack.enter_context(self.nc.semaphore(name))

    def _deps(self, eng, reads, writes):
        need = {}

        def add(sv):
            if sv is None:
                return
            s, v = sv
            k = id(s)
            if k not in need or need[k][1] < v:
                need[k] = (s, v)
        for b in reads:
            add(b.lw)
        for b in writes:
            add(b.lw)
            for k, sv in b.rd.items():
                add(sv)
        waits = []
        w = self.waited[eng]
        for k, (s, v) in need.items():
            if eng == "pe" and s is self.esem["pe"]:
                continue
            if w.get(k, 0) < v:
                w[k] = v
                waits.append((s, v))
        return waits

    def _mark(self, tag, reads, writes):
        for b in reads:
            k = id(tag[0])
            if k not in b.rd or b.rd[k][1] < tag[1]:
                b.rd[k] = tag
        for b in writes:
            b.lw = tag
            b.rd = {}

    def op(self, eng, fn, reads=(), writes=(), signal=True):
        waits = self._deps(eng, reads, writes)
        s = self.esem[eng]
        idx = self.ecnt[eng] + 1
        if signal:
            self.ecnt[eng] = idx
        self.streams[eng].append((waits, fn, (s, 1) if signal else None))
        self._mark((s, idx), reads, writes)

    def dma(self, queue, fn, reads, writes, sembuf):
        waits = self._deps(queue, reads, writes)
        sembuf.semval += 16
        self.streams[queue].append((waits, fn, (sembuf.sem, 16)))
        self._mark((sembuf.sem, sembuf.semval), reads, writes)

    def finish(self, queue, bufs):
        waits = self._deps(queue, bufs, bufs)
        self.streams[queue].append((waits, None, None))

    def emit(self):
        nc = self.nc
        with nc.Block() as block:
            def run(e, name):
                for waits, fn, inc in self.streams[name]:
                    for (s, v) in waits:
                        e.wait_ge(s, self.bigval() if v == BIG else v)
                    if fn is not None:
                        ins = fn(e)
                        if inc is not None:
                            ins.then_inc(inc[0], inc[1])

            @block.tensor
            def _(e):
                run(e, "pe")

            @block.scalar
            def _(e):
                run(e, "act")

            @block.vector
            def _(e):
                run(e, "dve")

            @block.gpsimd
            def _(e):
                run(e, "pool")

            @block.sync
            def _(e):
                run(e, "sp")


class WStream:
    def __init__(self, S, nc):
        self.S = S
        self.nc = nc
        self.order = []
        self.frozen = False
        self.cursor = 0
        self.passno = 0
        self.slots = []
        for i in range(NSLOT):
            t = nc.alloc_sbuf_tensor(f"wslot{i}", [128, WCOLS], BF16)
            self.slots.append(Buf(f"wslot{i}", t, S.new_sem(f"wslot{i}")))
        self.issued = -1
        self.wbf = None
        self.wf32 = None
        self.castbuf = Buf("wcast", None, S.new_sem("wcast"))
        self.nb = None
        self.npass = None

    def set_npass(self, n):
        self.npass = n

    def nblocks(self):
        return (len(self.order) + TPB - 1) // TPB

    def _issue(self, gb):
        S = self.S
        slot = self.slots[gb % NSLOT]
        me = self

        def fn(e, gb=gb, slot=slot):
            b = gb % me.nblocks()
            return e.dma_start(out=slot.t[:], in_=me.wbf[b])
        S.dma("sp", fn, reads=[self.castbuf], writes=[slot], sembuf=slot)

    def align(self, n):
        if not self.frozen:
            while len(self.order) % n:
                self.order.append(None)
            self.cursor = len(self.order)
        else:
            while self.cursor % n:
                assert self.order[self.cursor] is None
                self.cursor += 1

    def tile(self, key):
        if not self.frozen:
            self.order.append(key)
        else:
            assert self.order[self.cursor] == key, (self.order[self.cursor], key)
        pos = self.cursor
        self.cursor += 1
        blk = pos // TPB
        gb = self.passno * (self.nblocks() if self.frozen else 0) + blk
        if self.frozen:
            lim = min(gb + PREFETCH, self.npass * self.nblocks() - 1)
        else:
            lim = gb
        while self.issued < lim:
            self.issued += 1
            self._issue(self.issued)
        return self.slots[gb % NSLOT], (pos % TPB) * 128

    def end_pass(self):
        if not self.frozen:
            self.frozen = True
        assert self.cursor == len(self.order)
        self.cursor = 0
        self.passno += 1


class Prog:
    def __init__(self, stok, stage="full"):
        assert stok % T == 0
        self.stok = stok
        self.nt = stok // T
        self.stage = stage
        self.stack = contextlib.ExitStack()
        nc = self.nc = bass.Bass("TRN2", target_bir_lowering=False)
        self.S = Sched(nc, self.stack)
        self.prm_cols = {}
        self.prm_list = []
        self.rot = list(range(8))
        self.rot_i = 0
        self.rotg = [0] * NG
        self.build()

    def prm(self, name, ncols):
        if name not in self.prm_cols:
            c = sum(n for _, n in self.prm_list)
            self.prm_cols[name] = c
            self.prm_list.append((name, ncols))
            assert c + ncols <= NPRM
        return self.prm_cols[name]

    def pcol(self, name, ncols, j=0, rows=128):
        c = self.prm(name, ncols) + j
        return self.PRM.t[0:rows, c:c + 1]

    def pbc(self, name, n):
        c = self.prm(name, 8)
        return bass.AP(self.PRM.t, c, [[NPRM, 64], [1, 8], [0, n]])

    def sb(self, name, shape, dt, sem=False):
        t = self.nc.alloc_sbuf_tensor("sb_" + name, shape, dt)
        return Buf(name, t, self.S.new_sem("d_" + name) if sem else None)

    def rb(self):
        b = self.P[self.rot[self.rot_i % len(self.rot)]]
        self.rot_i += 1
        return b

    def wt(self, name, kc, mj):
        return self.W.tile((name, kc, mj))

    def mm(self, ob, oap, lb, lap, rb_, rap, start, stop, sig=None):
        self.S.op("pe", lambda e: e.matmul(oap, lhsT=lap, rhs=rap, start=start, stop=stop),
                  reads=[lb, rb_], writes=[ob], signal=stop if sig is None else sig)

    def tp(self, ob, oap, ib, iap, nrow):
        idb = self.identb
        self.S.op("pe", lambda e: e.transpose(oap, iap, idb.t[0:nrow, 0:nrow]), reads=[ib, idb], writes=[ob])

    def tt(self, eng, out, in0, in1, op, R, Wr):
        self.S.op(eng, lambda e: e.tensor_tensor(out=out, in0=in0, in1=in1, op=op), R, Wr)

    def stt(self, eng, out, in0, scalar, in1, op0, op1, R, Wr):
        self.S.op(eng, lambda e: e.scalar_tensor_tensor(out=out, in0=in0, scalar=scalar, in1=in1, op0=op0, op1=op1), R, Wr)

    def ts(self, eng, out, in0, s1, s2, op0, op1, R, Wr):
        if s2 is None:
            self.S.op(eng, lambda e: e.tensor_scalar(out=out, in0=in0, scalar1=s1, scalar2=None, op0=op0), R, Wr)
        else:
            self.S.op(eng, lambda e: e.tensor_scalar(out=out, in0=in0, scalar1=s1, scalar2=s2, op0=op0, op1=op1), R, Wr)

    def act(self, out, in_, func, R, Wr, **kw):
        self.S.op("act", lambda e: e.activation(out=out, in_=in_, func=func, **kw), R, Wr)

    def cp(self, eng, out, in_, R, Wr):
        if eng == "act":
            self.act(out, in_, AF.Copy, R, Wr)
        else:
            self.S.op(eng, lambda e: e.tensor_copy(out=out, in_=in_), R, Wr)

    def rcp(self, out, in_, R, Wr):
        self.S.op("dve", lambda e: e.reciprocal(out=out, in_=in_), R, Wr)

    def dma(self, q, out, in_, R, Wr, sb_):
        self.S.dma(q, lambda e: e.dma_start(out=out, in_=in_), R, Wr, sb_)

    def build(self):
        nc, S = self.nc, self.S
        stok = self.stok
        dt_in = lambda n, sh: nc.dram_tensor(n, sh, F32, kind="ExternalInput").ap()
        self.xT = dt_in("xT", [D, stok])
        self.outT = nc.dram_tensor("outT", [D, stok], F32, kind="ExternalOutput").ap()
        self.prm_d = dt_in("prm", [128, NPRM])
        self.rope_d = dt_in("rope", [4, 32, stok])
        self.ident_d = dt_in("ident", [128, 128])
        self.cmask_d = dt_in("cmask", [64, 3, 64])
        self.amask_d = dt_in("amask", [128, 2, T])
        self.lora_d = dt_in("lora", [128, 4, 8, 64])
        self.Kscr_d = nc.dram_tensor("Kscr", [8, 64, stok], BF16, kind="Internal").ap()
        self.Kpe_d = nc.dram_tensor("Kpescr", [32, stok], BF16, kind="Internal").ap()
        self.Vscr_d = nc.dram_tensor("Vscr", [8, 128, stok // 128, 65], BF16, kind="Internal").ap()
        self.Kscr = Buf("Kscr")
        self.Kpescr = Buf("Kpescr")
        self.Vscr = Buf("Vscr")
        self.W = WStream(S, nc)
        self.W.set_npass(self.nt)
        W = self.W
        S.bigval = lambda: 16 * W.nblocks()

        self.P = [Buf(f"ps{i}", nc.alloc_psum_tensor(f"ps{i}", [128, 512], F32)) for i in range(8)]
        for b in self.P:
            b.tb = b.t.bitcast(BF16)
        sb = self.sb
        self.PRM = sb("PRM", [128, NPRM], F32, sem=True)
        self.ones32 = sb("ones32", [128, 128], F32)
        self.identb = sb("identb", [128, 128], BF16, sem=True)
        self.cmask = sb("cmask", [64, 3, 64], BF16, sem=True)
        self.amask = sb("amask", [128, 2, T], BF16, sem=True)
        self.lora = sb("lora", [128, 4, 8, 64], BF16, sem=True)
        self.rope = sb("rope", [128, 2, T], F32, sem=True)
        self.x32 = sb("x32", [128, NKC, T], F32, sem=True)
        self.xb = sb("xb", [128, NKC, T], BF16, sem=True)
        self.h1 = sb("h1", [128, NKC, T], F32, sem=True)
        self.h1b = sb("h1b", [128, NKC, T], BF16)
        self.g = sb("g", [128, NJ, T], BF16)
        self.tmp = [sb(f"tmp{i}", [128, T], F32) for i in range(6)]
        self.mean = sb("mean", [128, T], F32)
        self.rstd = sb("rstd", [128, T], F32)
        self.mixin = sb("mixin", [128, NKC, T], BF16)
        self.qlat = sb("qlat", [128, 3, T], F32)
        self.qn = sb("qn", [128, 3, T], BF16)
        self.kvlat = sb("kvlat", [128, 2, T], F32)
        self.kvn = sb("kvn", [128, 2, T], BF16)
        self.Qall = sb("Qall", [128, 8, T], BF16)
        self.Kst = sb("Kst", [64, 8, T], BF16, sem=True)
        self.kpe = sb("kpe", [32, T], BF16, sem=True)
        self.Vst = sb("Vst", [128, 8, T // 128, 65], BF16, sem=True)
        self.Kt = [sb(f"Kt{i}", [128, KSEG], BF16, sem=True) for i in range(2)]
        self.Vt = [sb(f"Vt{i}", [128, KSEG // 128, 65], BF16, sem=True) for i in range(2)]
        self.PT = [sb(f"PT{i}", [128, T], BF16) for i in range(3)]
        self.ymla = sb("ymla", [128, 8, T], BF16)
        self.rec = sb("rec", [128, T], F32)
        self.R32 = sb("R32", [64, 8, T], F32, sem=True)
        self.K32 = sb("K32", [64, 8, T], F32)
        self.V32 = sb("V32", [64, 8, T], F32)
        self.bigtmp = sb("bigtmp", [64, 8, T], F32, sem=True)
        self.LR = sb("LR", [64, 8, 1], F32)
        self.LK = sb("LK", [64, 8, 1], F32)
        self.LV = sb("LV", [64, 8, 1], F32)
        self.WD = sb("WD", [64, T], F32)
        self.AD = sb("AD", [64, T], F32)
        self.G1 = sb("G1", [128, T], F32)
        self.G2 = sb("G2", [32, T], F32)
        self.Lsm = sb("Lsm", [128, 4], F32)
        self.tw = sb("tw", [64, T], BF16)
        self.adb = sb("adb", [64, T], BF16)
        self.sg1 = sb("sg1", [128, T], BF16)
        self.sg2 = sb("sg2", [32, T], BF16)
        self.yrw = sb("yrw", [128, 8, T], BF16)
        self.cg = []
        for gi in range(NG):
            d = {}
            for n in ("SIG", "A32", "KKN", "B32", "KM", "BON", "E", "ct0", "ct1", "ct2", "S32", "tmpS", "Y32"):
                d[n] = sb(f"{n}_{gi}", [64, HG, CH], F32)
            for n in ("AT", "RT", "KT", "BT", "KH", "BH", "VB", "GATE", "Nm0", "Nm1", "Mm0", "Mm1", "Qm0", "Qm1",
                      "AKT", "RKT", "RBT", "Vtm", "KHtm", "BHtm", "P1s", "Us", "Sb0", "Sb1"):
                d[n] = sb(f"{n}_{gi}", [64, HG, CH], BF16)
            d["GC"] = sb(f"GC_{gi}", [64, HG], F32)
            self.cg.append(d)
        self.chunk_no = 0

        self.dma("pool", self.PRM.t[:], self.prm_d, [], [self.PRM], self.PRM)
        self.dma("pool", self.identb.t[:], self.ident_d, [], [self.identb], self.identb)
        self.dma("pool", self.cmask.t[:], self.cmask_d, [], [self.cmask], self.cmask)
        self.dma("pool", self.amask.t[:], self.amask_d, [], [self.amask], self.amask)
        self.dma("pool", self.lora.t[:], self.lora_d, [], [self.lora], self.lora)
        S.op("pool", lambda e: e.memset(self.ones32.t[:], 1.0), [], [self.ones32])
        for b in [self.LR, self.LK, self.LV, self.Lsm] + [d["S32"] for d in self.cg] + [d["Sb0"] for d in self.cg]:
            S.op("pool", lambda e, b=b: e.memset(b.t[:], 0.0), [], [b])
        S.op("pool", lambda e: e.memset(self.Vst.t[:], 1.0), [], [self.Vst])
        for b in [self.ymla, self.yrw, self.Qall] + self.Kt:
            S.op("pool", lambda e, b=b: e.memset(b.t[:], 0.0), [], [b])
        ka = self.prm("k_a", 8)
        om = self.prm("omka", 8)
        self.ts("dve", self.PRM.t[0:64, om:om + 8], self.PRM.t[0:64, ka:ka + 8], -1.0, 1.0, ALU.mult, ALU.add, [self.PRM], [self.PRM])

        def castfn(e):
            ins = None
            for b in range(W.nblocks()):
                if b % 8 == 0 and b > 0:
                    e.wait_ge(W.castbuf.sem, 16 * b)
                ins = e.dma_start(out=W.wbf[b], in_=W.wf32[b])
                if b < W.nblocks() - 1:
                    ins.then_inc(W.castbuf.sem, 16)
            return ins
        S.streams["pool"].append(([], castfn, (W.castbuf.sem, 16)))
        W.castbuf.lw = (W.castbuf.sem, BIG)

        for it in range(self.nt):
            self.tile_body(it)
            W.end_pass()
        S.finish("pool", [self.h1, self.x32, self.R32, self.bigtmp])

        nb = W.nblocks()
        self.nb = nb
        W.wf32 = nc.dram_tensor("wf32", [nb, 128, WCOLS], F32, kind="ExternalInput").ap()
        W.wbf = nc.dram_tensor("wbf", [nb, 128, WCOLS], BF16, kind="Internal").ap()
        S.emit()

    def ln_stats(self, z, mj):
        S1, S2 = self.P[6], self.P[7]
        sq = self.tmp[mj % 2]
        self.act(sq.t[:], z.t[:, mj, :], AF.Square, [z], [sq])
        self.mm(S1, S1.t[:, 0:T], self.ones32, self.ones32.t[:], z, z.t[:, mj, :], mj == 0, mj == NKC - 1)
        self.mm(S2, S2.t[:, 0:T], self.ones32, self.ones32.t[:], sq, sq.t[:], mj == 0, mj == NKC - 1)

    def ln_apply(self, z, gname, bname, out32, outb, eps):
        S1, S2 = self.P[6], self.P[7]
        mean, rstd = self.mean, self.rstd
        self.act(mean.t[:], S1.t[:, 0:T], AF.Copy, [S1], [mean], scale=1.0 / D)
        self.tt("dve", rstd.t[:], mean.t[:], mean.t[:], ALU.mult, [mean], [rstd])
        self.stt("dve", rstd.t[:], S2.t[:, 0:T], 1.0 / D, rstd.t[:], ALU.mult, ALU.subtract, [S2, rstd], [rstd])
        self.rsqrt_(rstd, eps)
        for mj in range(NKC):
            tn = self.tmp[3 + mj % 2]
            self.tt("dve", tn.t[:], z.t[:, mj, :], mean.t[:], ALU.subtract, [z, mean], [tn])
            self.tt("dve", tn.t[:], tn.t[:], rstd.t[:], ALU.mult, [tn, rstd], [tn])
            self.act(out32.t[:, mj, :], tn.t[:], AF.Identity, [tn, self.PRM], [out32],
                     bias=self.pcol(bname, NKC, mj), scale=self.pcol(gname, NKC, mj))
            if outb is not None:
                self.act(outb.t[:, mj, :], tn.t[:], AF.Identity, [tn, self.PRM], [outb],
                         bias=self.pcol(bname, NKC, mj), scale=self.pcol(gname, NKC, mj))

    def ffn_ln(self, xin, xb, wn, gname, bname, out32, outb, after_w13=None):
        P = self.P
        for j in range(NJ):
            rows = 128 if j < NJ - 1 else DFF - 128 * (NJ - 1)
            A, B = P[(j % 2) * 2], P[(j % 2) * 2 + 1]
            for (bank, nm) in ((A, wn + "1"), (B, wn + "3")):
                for kc in range(NKC):
                    slot, off = self.wt(nm, kc, j)
                    self.mm(bank, bank.t[0:rows, 0:T], slot, slot.t[:, off:off + rows], xb, xb.t[:, kc, :], kc == 0, kc == NKC - 1)
            sl = self.tmp[j % 2]
            self.act(sl.t[0:rows, :], A.t[0:rows, 0:T], AF.Silu, [A], [sl])
            self.tt("dve", self.g.t[0:rows, j, :], sl.t[0:rows, :], B.t[0:rows, 0:T], ALU.mult, [B, sl], [self.g])
        if after_w13 is not None:
            after_w13()
        for mj in range(NKC):
            bank = P[4 + mj % 2]
            for kc in range(NJ):
                rows = 128 if kc < NJ - 1 else DFF - 128 * (NJ - 1)
                slot, off = self.wt(wn + "2", kc, mj)
                self.mm(bank, bank.t[:, 0:T], slot, slot.t[0:rows, off:off + 128], self.g, self.g.t[0:rows, kc, :], kc == 0, kc == NJ - 1)
            self.stt("dve", xin.t[:, mj, :], bank.t[:, 0:T], 0.5 / ALPHA, xin.t[:, mj, :], ALU.mult, ALU.add, [bank, xin], [xin])
            if mj > 0:
                self.ln_stats(xin, mj - 1)
        self.ln_stats(xin, NKC - 1)
        self.ln_apply(xin, gname, bname, out32, outb, LN_EPS / (ALPHA * ALPHA))

    def rms_proj(self, wname, nch, lat, outn, gname, dim):
        SS = self.P[7]
        for c in range(nch):
            bank = self.rb()
            for kc in range(NKC):
                slot, off = self.wt(wname, kc, c)
                self.mm(bank, bank.t[:, 0:T], slot, slot.t[:, off:off + 128], self.h1b, self.h1b.t[:, kc, :], kc == 0, kc == NKC - 1)
            self.cp("dve", lat.t[:, c, :], bank.t[:, 0:T], [bank], [lat])
            sq = self.tmp[c % 2]
            self.act(sq.t[:], lat.t[:, c, :], AF.Square, [lat], [sq])
            self.mm(SS, SS.t[:, 0:T], self.ones32, self.ones32.t[:], sq, sq.t[:], c == 0, c == nch - 1)
        t0, rs = self.tmp[2], self.tmp[3]
        self.ts("dve", rs.t[:], SS.t[:, 0:T], 1.0 / dim, 1e-6, ALU.mult, ALU.add, [SS], [rs])
        self.act(rs.t[:], rs.t[:], AF.Ln, [rs], [rs])
        self.act(rs.t[:], rs.t[:], AF.Exp, [rs], [rs], scale=-0.5)
        import os
        if os.environ.get("KSTT", "1") == "0":
            return
        for c in range(nch):
            self.stt("dve", outn.t[:, c, :], lat.t[:, c, :], self.pcol(gname, nch, c), rs.t[:], ALU.mult, ALU.mult, [lat, rs, self.PRM], [outn])

    def mla_tile(self, it):
        S, P = self.S, self.P
        t0 = it * T
        self.rot = [0, 1, 2, 3, 4, 5]
        self.dma("pool", self.rope.t[0:32, :, :], self.rope_d[0:2, :, t0:t0 + T].rearrange("a p t -> p a t"), [], [self.rope], self.rope)
        self.dma("pool", self.rope.t[64:96, :, :], self.rope_d[2:4, :, t0:t0 + T].rearrange("a p t -> p a t"), [], [self.rope], self.rope)
        import os
        if int(os.environ.get("KDBG", "9")) < 1:
            return
        self.rms_proj("win_q", 3, self.qlat, self.qn, "qn_g", 384.0)
        self.rms_proj("win_kv", 2, self.kvlat, self.kvn, "kvn_g", 256.0)
        import os
        if int(os.environ.get("KDBG", "9")) < 2:
            return
        bk = self.rb()
        for i, nm in enumerate(("win_kpe", "win_kpes")):
            for kc in range(NKC):
                slot, off = self.wt(nm, kc, 0)
                self.mm(bk, bk.t[0:32, i * T:(i + 1) * T], slot, slot.t[:, off:off + 32], self.h1b, self.h1b.t[:, kc, :], kc == 0, kc == NKC - 1)
        ta, tb_ = self.tmp[4], self.tmp[5]
        self.tt("dve", ta.t[0:32, :], bk.t[0:32, 0:T], self.rope.t[0:32, 0, :], ALU.mult, [bk, self.rope], [ta])
        self.tt("dve", tb_.t[0:32, :], bk.t[0:32, T:2 * T], self.rope.t[0:32, 1, :], ALU.mult, [bk, self.rope], [tb_])
        self.tt("pool", self.kpe.t[:], ta.t[0:32, :], tb_.t[0:32, :], ALU.add, [ta, tb_], [self.kpe])
        self.dma("pool", self.Kpe_d[:, t0:t0 + T], self.kpe.t[:], [self.kpe], [self.Kpescr], self.kpe)
        if int(os.environ.get("KDBG", "9")) < 3:
            return
        for h in range(8):
            bm, bs = self.rb(), self.rb()
            for (bank, nm) in ((bm, "wq_main"), (bs, "wq_swap")):
                for c in range(3):
                    slot, off = self.wt(nm, c, h)
                    self.mm(bank, bank.t[0:96, 0:T], slot, slot.t[:, off:off + 96], self.qn, self.qn.t[:, c, :], c == 0, c == 2)
            self.ts("dve", self.Qall.t[0:64, h, :], bm.t[0:64, 0:T], QSCALE, None, ALU.mult, None, [bm], [self.Qall])
            self.tt("dve", ta.t[64:96, :], bm.t[64:96, 0:T], self.rope.t[64:96, 0, :], ALU.mult, [bm, self.rope], [ta])
            self.tt("dve", tb_.t[64:96, :], bs.t[64:96, 0:T], self.rope.t[64:96, 1, :], ALU.mult, [bs, self.rope], [tb_])
            self.tt("pool", self.Qall.t[64:96, h, :], ta.t[64:96, :], tb_.t[64:96, :], ALU.add, [ta, tb_], [self.Qall])
        if int(os.environ.get("KDBG", "9")) < 4:
            return
        for h in range(8):
            bank = self.rb()
            for c in range(2):
                slot, off = self.wt("wkv_k", c, h)
                self.mm(bank, bank.t[0:64, 0:T], slot, slot.t[:, off:off + 64], self.kvn, self.kvn.t[:, c, :], c == 0, c == 1)
            self.cp("act", self.Kst.t[:, h, :], bank.t[0:64, 0:T], [bank], [self.Kst])
        self.dma("pool", self.Kscr_d[:, :, t0:t0 + T].rearrange("h r s -> r h s"), self.Kst.t[:], [self.Kst], [self.Kscr], self.Kst)
        for tb in range(T // 128):
            bank = self.rb()
            self.W.align(4)
            for c in range(2):
                sl0 = None
                for q in range(4):
                    slot, off = self.wt("wkv_v", c, q)
                    if q == 0:
                        sl0, off0 = slot, off
                    assert slot is sl0
                self.mm(bank, bank.t[:, :], self.kvn, self.kvn.t[:, c, tb * 128:(tb + 1) * 128], sl0, sl0.t[:, off0:off0 + 512], c == 0, c == 1)
            self.cp("act", self.Vst.t[:, :, tb, 0:64], bank.t[:, :].rearrange("p (h d) -> p h d", h=8), [bank], [self.Vst])
        self.dma("pool", self.Vscr_d[:, :, t0 // 128:t0 // 128 + T // 128, :].rearrange("h p k e -> p h k e"), self.Vst.t[:], [self.Vst], [self.Vscr], self.Vst)

    def attn_gen(self, it, every=1):
        P = self.P
        t0 = it * T
        nkb = (t0 + T) // 128
        SEGB = KSEG // 128
        nseg = (nkb + SEGB - 1) // SEGB
        items = [(h, sg) for h in range(8) for sg in range(nseg)]
        units = []
        for i, (h, sg) in enumerate(items):
            for kb in range(sg * SEGB, min(nkb, (sg + 1) * SEGB)):
                units.append((i, h, sg, kb))

        def load(i):
            h, sg = items[i]
            k0 = sg * KSEG
            nk = min(KSEG, t0 + T - k0)
            Kt, Vt = self.Kt[i % 2], self.Vt[i % 2]
            self.dma("pool", Kt.t[0:64, 0:nk], self.Kscr_d[h, :, k0:k0 + nk], [self.Kscr], [Kt], Kt)
            self.dma("pool", Kt.t[64:96, 0:nk], self.Kpe_d[:, k0:k0 + nk], [self.Kpescr], [Kt], Kt)
            self.dma("pool", Vt.t[:, 0:nk // 128, :], self.Vscr_d[h, :, k0 // 128:(k0 + nk) // 128, :], [self.Vscr], [Vt], Vt)

        cnt = [0]

        def bank():
            b = P[cnt[0] % 3]
            cnt[0] += 1
            return b

        def geom(u):
            i, h, sg, kb = u
            kl = kb - sg * SEGB
            loc = kb - (nkb - T // 128)
            q0 = max(loc, 0) * 128
            return kl, loc, q0

        scb = {}

        def emit_score(n):
            i, h, sg, kb = units[n]
            kl, loc, q0 = geom(units[n])
            Kt = self.Kt[i % 2]
            sc = bank()
            scb[n] = sc
            self.mm(sc, sc.t[:, q0:T], Kt, Kt.t[:, kl * 128:(kl + 1) * 128], self.Qall, self.Qall.t[:, h, q0:T], True, True)

        load(0)
        emit_score(0)
        emit_score(1)
        for n, u in enumerate(units):
            i, h, sg, kb = u
            kl, loc, q0 = geom(u)
            if kb == sg * SEGB and i + 1 < len(items):
                load(i + 1)
            if n + 2 < len(units):
                emit_score(n + 2)
            Vt = self.Vt[i % 2]
            O = P[3]
            sc = scb.pop(n)
            pt = self.PT[n % 3]
            self.act(pt.t[:, q0:T], sc.t[:, q0:T], AF.Exp, [sc], [pt])
            if loc >= 0:
                self.tt("pool", pt.t[:, q0:T], pt.t[:, q0:T], self.amask.t[:, loc, q0:T], ALU.mult, [pt, self.amask], [pt])
            self.mm(O, O.t[0:65, q0:T], Vt, Vt.t[:, kl, :], pt, pt.t[:, q0:T], kb == 0, kb == nkb - 1, sig=True)
            if kb == nkb - 1:
                rec, to, bc = self.rec, self.tmp[h % 2], sc
                self.cp("act", to.t[0:65, :], O.t[0:65, 0:T], [O], [to])
                self.rcp(rec.t[64:65, :], to.t[64:65, :], [to], [rec])
                self.mm(bc, bc.t[0:64, 0:T], self.ones32, self.ones32.t[64:65, 0:64], rec, rec.t[64:65, :], True, True)
                self.tt("dve", self.ymla.t[0:64, h, :], to.t[0:64, :], bc.t[0:64, 0:T], ALU.mult, [to, bc], [self.ymla])
            if (n + 1) % every == 0:
                yield

    def shift3(self, X, L, muname):
        d = self.bigtmp
        self.tt("dve", d.t[:, :, 1:T], X.t[:, :, 0:T - 1], X.t[:, :, 1:T], ALU.subtract, [X], [d])
        self.tt("pool", d.t[:, :, 0:1], L.t[:], X.t[:, :, 0:1], ALU.subtract, [X, L], [d])
        self.cp("pool", L.t[:], X.t[:, :, T - 1:T], [X], [L])
        self.tt("dve", d.t[:], d.t[:], self.pbc(muname, T), ALU.mult, [d, self.PRM], [d])
        self.tt("dve", X.t[:], X.t[:], d.t[:], ALU.add, [X, d], [X])

    def shift2(self, X, rows, lcol, muname):
        d = self.tmp[2]
        L = self.Lsm
        self.tt("dve", d.t[0:rows, 1:T], X.t[0:rows, 0:T - 1], X.t[0:rows, 1:T], ALU.subtract, [X], [d])
        self.tt("pool", d.t[0:rows, 0:1], L.t[0:rows, lcol:lcol + 1], X.t[0:rows, 0:1], ALU.subtract, [X, L], [d])
        self.cp("pool", L.t[0:rows, lcol:lcol + 1], X.t[0:rows, T - 1:T], [X], [L])
        self.stt("dve", X.t[0:rows, :], d.t[0:rows, :], self.pcol(muname, 1, 0, rows), X.t[0:rows, :], ALU.mult, ALU.add, [d, X, self.PRM], [X])

    def rwkv_tile(self, it):
        P = self.P
        self.rot = list(range(8))
        h1b = self.h1b
        for nm, X in (("win_r", self.R32), ("win_k", self.K32), ("win_v", self.V32)):
            for hp in range(4):
                bank = self.rb()
                for hh in range(2):
                    h = hp * 2 + hh
                    for kc in range(NKC):
                        slot, off = self.wt(nm, kc, h)
                        self.mm(bank, bank.t[0:64, hh * T:(hh + 1) * T], slot, slot.t[:, off:off + 64], h1b, h1b.t[:, kc, :], kc == 0, kc == NKC - 1)
                self.cp("act", X.t[:, hp * 2:hp * 2 + 2, :], bank.t[0:64, :].rearrange("p (h t) -> p h t", h=2), [bank], [X])
        bA, bB = self.rb(), self.rb()
        for (bank, lo, rows, nm, mj, X) in ((bA, 0, 64, "win_wd", 0, self.WD), (bA, T, 64, "win_ad", 0, self.AD),
                                            (bB, 0, 128, "win_gd", 0, self.G1), (bB, T, 32, "win_gd", 1, self.G2)):
            for kc in range(NKC):
                slot, off = self.wt(nm, kc, mj)
                self.mm(bank, bank.t[0:rows, lo:lo + T], slot, slot.t[:, off:off + rows], h1b, h1b.t[:, kc, :], kc == 0, kc == NKC - 1)
            self.cp("act", X.t[0:rows, :], bank.t[0:rows, lo:lo + T], [bank], [X])
        self.shift3(self.R32, self.LR, "mu_r")
        self.shift3(self.K32, self.LK, "mu_k")
        self.shift3(self.V32, self.LV, "mu_v")
        self.shift2(self.WD, 64, 0, "mu_wd")
        self.shift2(self.AD, 64, 1, "mu_ad")
        self.shift2(self.G1, 128, 2, "mu_g1")
        self.shift2(self.G2, 32, 3, "mu_g2")
        self.act(self.tw.t[:], self.WD.t[:], AF.Tanh, [self.WD], [self.tw])
        self.cp("pool", self.adb.t[:], self.AD.t[:], [self.AD], [self.adb])
        self.act(self.sg1.t[:], self.G1.t[:], AF.Sigmoid, [self.G1], [self.sg1])
        self.act(self.sg2.t[:], self.G2.t[:], AF.Sigmoid, [self.G2], [self.sg2])

    def rwkv_chunks_gen(self):
        for ci in range(T // CH):
            yield from inter([self.rwkv_chunk(ci, gi) for gi in range(NG)])
        self.chunk_no += T // CH

    def pbg(self, name, gi, n):
        c = self.prm(name, 8) + gi * HG
        return bass.AP(self.PRM.t, c, [[NPRM, 64], [1, HG], [0, n]])

    def rbg(self, gi):
        nb = 4 // NG
        b = self.P[4 + gi * nb + self.rotg[gi] % nb]
        self.rotg[gi] += 1
        return b

    def rsqrt_(self, buf, eps):
        self.ts("dve", buf.t[:], buf.t[:], 0.0, eps, ALU.max, ALU.add, [buf], [buf])
        self.act(buf.t[:], buf.t[:], AF.Ln, [buf], [buf])
        self.act(buf.t[:], buf.t[:], AF.Exp, [buf], [buf], scale=-0.5)

    def rwkv_chunk(self, ci, gi):
        cs = slice(ci * CH, (ci + 1) * CH)
        h0 = gi * HG
        G = self.cg[gi]
        Rc, Kc, Vc = self.R32.t[:, h0:h0 + HG, cs], self.K32.t[:, h0:h0 + HG, cs], self.V32.t[:, h0:h0 + HG, cs]
        RR, KK, VV = [self.R32], [self.K32], [self.V32]
        PR = self.PRM
        lora = self.lora
        SIG, A32, KKN, B32, KM, BON, E = [G[n] for n in ("SIG", "A32", "KKN", "B32", "KM", "BON", "E")]
        ct0, ct1, ct2 = G["ct0"], G["ct1"], G["ct2"]
        AT, RT, KT, BT, KH, BH, VB, GATE = [G[n] for n in ("AT", "RT", "KT", "BT", "KH", "BH", "VB", "GATE")]
        W_ = HG * CH
        v8 = lambda bank: bank.t[0:64, 0:W_].rearrange("p (h t) -> p h t", h=HG)
        v8b = lambda bank: bank.tb[0:64, 0:W_].rearrange("p (h t) -> p h t", h=HG)
        flat = lambda b: b.t[:].rearrange("p h t -> p (h t)")
        ones64 = self.ones32.t[0:64, 0:64]
        rb = lambda: self.rbg(gi)
        bc = lambda name: self.pbg(name, gi, CH)
        b1 = rb()
        for h in range(HG):
            self.mm(b1, b1.t[0:64, h * CH:(h + 1) * CH], lora, lora.t[0:64, 0, h0 + h, :], self.tw, self.tw.t[:, cs], True, True)
        yield
        self.tt("dve", SIG.t[:], v8(b1), bc("w0"), ALU.add, [b1, PR], [SIG])
        yield
        self.act(SIG.t[:], SIG.t[:], AF.Exp, [SIG], [SIG], scale=-1.0)
        self.act(SIG.t[:], SIG.t[:], AF.Ln, [SIG], [SIG], bias=1.0)
        self.act(SIG.t[:], SIG.t[:], AF.Exp, [SIG], [SIG], scale=-1.0)
        yield
        b2 = rb()
        for h in range(HG):
            self.mm(b2, b2.t[0:64, h * CH:(h + 1) * CH], lora, lora.t[0:64, 1, h0 + h, :], self.adb, self.adb.t[:, cs], True, True)
        yield
        self.tt("dve", A32.t[:], v8(b2), bc("a0"), ALU.add, [b2, PR], [A32])
        yield
        self.act(A32.t[:], A32.t[:], AF.Exp, [A32], [A32], scale=-1.0)
        self.act(A32.t[:], A32.t[:], AF.Ln, [A32], [A32], bias=1.0)
        self.act(A32.t[:], A32.t[:], AF.Exp, [A32], [A32], scale=-1.0)
        yield
        b3 = rb()
        for h in range(HG):
            self.mm(b3, b3.t[0:64, h * CH:(h + 1) * CH], lora, lora.t[:, 2, h0 + h, :], self.sg1, self.sg1.t[:, cs], True, False)
            self.mm(b3, b3.t[0:64, h * CH:(h + 1) * CH], lora, lora.t[0:32, 3, h0 + h, :], self.sg2, self.sg2.t[0:32, cs], False, True)
        yield
        self.cp("act", GATE.t[:], v8(b3), [b3], [GATE])
        yield
        self.tt("dve", KKN.t[:], Kc, bc("k_k"), ALU.mult, KK + [PR], [KKN])
        yield
        self.act(ct0.t[:], KKN.t[:], AF.Square, [KKN], [ct0])
        yield
        b4 = rb()
        self.mm(b4, b4.t[0:64, 0:W_], self.ones32, ones64, ct0, flat(ct0), True, True)
        yield
        self.cp("dve", ct1.t[:], v8(b4), [b4], [ct1])
        self.rsqrt_(ct1, 1e-24)
        yield
        self.tt("dve", KKN.t[:], KKN.t[:], ct1.t[:], ALU.mult, [KKN, ct1], [KKN])
        yield
        self.tt("dve", ct0.t[:], A32.t[:], bc("k_a"), ALU.mult, [A32, PR], [ct0])
        yield
        self.tt("dve", ct0.t[:], ct0.t[:], bc("omka"), ALU.add, [ct0, PR], [ct0])
        yield
        self.tt("dve", KM.t[:], Kc, ct0.t[:], ALU.mult, KK + [ct0], [KM])
        yield
        self.tt("pool", B32.t[:], KKN.t[:], A32.t[:], ALU.mult, [KKN, A32], [B32])
        yield
        self.tt("dve", ct0.t[:], Rc, KM.t[:], ALU.mult, RR + [KM], [ct0])
        yield
        self.tt("dve", ct0.t[:], ct0.t[:], bc("r_k"), ALU.mult, [ct0, PR], [ct0])
        yield
        b5 = rb()
        self.mm(b5, b5.t[0:64, 0:W_], self.ones32, ones64, ct0, flat(ct0), True, True)
        yield
        self.tt("dve", BON.t[:], v8(b5), Vc, ALU.mult, [b5] + VV, [BON])
        yield
        src, dst = SIG, ct2
        for d in (1, 2, 4, 8, 16, 32):
            self.tt("dve", dst.t[:, :, d:CH], src.t[:, :, d:CH], src.t[:, :, 0:CH - d], ALU.add, [src], [dst])
            self.cp("pool", dst.t[:, :, 0:d], src.t[:, :, 0:d], [src], [dst])
            src, dst = dst, src
            yield
        assert src is SIG
        self.act(E.t[:], SIG.t[:], AF.Exp, [SIG], [E], scale=-C0)
        yield
        self.tt("dve", RT.t[:], Rc, E.t[:], ALU.mult, RR + [E], [RT])
        yield
        self.stt("dve", AT.t[:, :, 1:CH], KKN.t[:, :, 1:CH], -1.0, E.t[:, :, 0:CH - 1], ALU.mult, ALU.mult, [KKN, E], [AT])
        yield
        self.ts("pool", AT.t[:, :, 0:1], KKN.t[:, :, 0:1], -1.0, None, ALU.mult, None, [KKN], [AT])
        self.cp("pool", G["GC"].t[:, :], E.t[:, :, CH - 1], [E], [G["GC"]])
        yield
        self.act(E.t[:], SIG.t[:], AF.Exp, [SIG], [E], scale=C0)
        yield
        self.tt("dve", KT.t[:], KM.t[:], E.t[:], ALU.mult, [KM, E], [KT])
        yield
        self.tt("pool", BT.t[:], B32.t[:], E.t[:], ALU.mult, [B32, E], [BT])
        yield
        last_bc = bass.AP(SIG.t, CH - 1, [[HG * CH, 64], [CH, HG], [0, CH]])
        self.tt("dve", ct0.t[:], last_bc, SIG.t[:], ALU.subtract, [SIG], [ct0])
        yield
        self.act(E.t[:], ct0.t[:], AF.Exp, [ct0], [E], scale=-C0)
        yield
        self.tt("dve", KH.t[:], KM.t[:], E.t[:], ALU.mult, [KM, E], [KH])
        yield
        self.tt("pool", BH.t[:], B32.t[:], E.t[:], ALU.mult, [B32, E], [BH])
        yield
        self.cp("act", VB.t[:], Vc, VV, [VB])
        yield
        cm = self.cmask
        mbc = lambda i: bass.AP(cm.t, i * 64, [[3 * 64, 64], [0, HG], [1, 64]])
        N0, M0, Q0 = G["Nm0"], G["Mm0"], G["Qm0"]
        for (L, R, dstb, mi) in ((AT, BT, N0, 0), (BT, AT, M0, 1), (KT, AT, G["AKT"], 1),
                                 (KT, RT, G["RKT"], 2), (BT, RT, G["RBT"], 2)):
            bank = rb()
            for h in range(HG):
                self.mm(bank, bank.t[0:64, h * CH:(h + 1) * CH], L, L.t[:, h, :], R, R.t[:, h, :], True, True)
            yield
            self.tt("dve", dstb.t[:], v8(bank), mbc(mi), ALU.mult, [bank, cm], [dstb])
            yield
        idbc = bass.AP(self.identb.t, 0, [[128, 64], [0, HG], [1, 64]])
        self.tt("dve", Q0.t[:], M0.t[:], idbc, ALU.add, [M0, self.identb], [Q0])
        yield
        cur = 0
        pend = None
        for j in range(1, 6):
            No, Mo = G[f"Nm{cur}"], G[f"Mm{cur}"]
            Nn, Mn = G[f"Nm{1 - cur}"], G[f"Mm{1 - cur}"]
            bn = rb()
            for h in range(HG):
                self.mm(bn, bn.t[0:64, h * CH:(h + 1) * CH], Mo, Mo.t[:, h, :], No, No.t[:, h, :], True, True)
            yield
            self.cp("act", Nn.t[:], v8(bn), [bn], [Nn])
            yield
            if j < 5:
                bm_ = rb()
                for h in range(HG):
                    self.mm(bm_, bm_.t[0:64, h * CH:(h + 1) * CH], No, No.t[:, h, :], Mo, Mo.t[:, h, :], True, True)
                yield
                self.cp("dve", Mn.t[:], v8(bm_), [bm_], [Mn])
                yield
            Qo, Qn = G[f"Qm{(j - 1) % 2}"], G[f"Qm{j % 2}"]
            bq = rb()
            for h in range(HG):
                self.mm(bq, bq.t[0:64, h * CH:(h + 1) * CH], Nn, Nn.t[:, h, :], Qo, Qo.t[:, h, :], True, True)
            yield
            self.tt("dve", Qn.t[:], v8(bq), Qo.t[:], ALU.add, [bq, Qo], [Qn])
            yield
            cur = 1 - cur
        cur = 5 % 2
        Qf = G[f"Qm{cur}"]
        for (X, Xtm) in ((VB, G["Vtm"]), (KH, G["KHtm"]), (BH, G["BHtm"])):
            bank = rb()
            for h in range(HG):
                self.tp(bank, bank.tb[0:64, h * CH:(h + 1) * CH], X, X.t[:, h, :], 64)
            yield
            self.cp("act", Xtm.t[:], v8b(bank), [bank], [Xtm])
            yield
        cn = self.chunk_no + ci
        Sb_cur, Sb_nxt = G[f"Sb{cn % 2}"], G[f"Sb{(cn + 1) % 2}"]
        S32, tmpS, Y, P1s, Us, Vtm = G["S32"], G["tmpS"], G["Y32"], G["P1s"], G["Us"], G["Vtm"]
        bp = rb()
        for h in range(HG):
            o = bp.t[0:64, h * CH:(h + 1) * CH]
            self.mm(bp, o, G["AKT"], G["AKT"].t[:, h, :], Vtm, Vtm.t[:, h, :], True, False)
            self.mm(bp, o, AT, AT.t[:, h, :], Sb_cur, Sb_cur.t[:, h, :], False, True)
        yield
        self.cp("act", P1s.t[:], v8(bp), [bp], [P1s])
        yield
        bu = rb()
        for h in range(HG):
            self.mm(bu, bu.t[0:64, h * CH:(h + 1) * CH], Qf, Qf.t[:, h, :], P1s, P1s.t[:, h, :], True, True)
        yield
        self.cp("dve", Us.t[:], v8(bu), [bu], [Us])
        yield
        bd = rb()
        for h in range(HG):
            o = bd.t[0:64, h * CH:(h + 1) * CH]
            self.mm(bd, o, G["KHtm"], G["KHtm"].t[:, h, :], Vtm, Vtm.t[:, h, :], True, False)
            self.mm(bd, o, G["BHtm"], G["BHtm"].t[:, h, :], Us, Us.t[:, h, :], False, True)
        by = rb()
        for h in range(HG):
            o = by.t[0:64, h * CH:(h + 1) * CH]
            self.mm(by, o, Sb_cur, Sb_cur.t[:, h, :], RT, RT.t[:, h, :], True, False)
            self.mm(by, o, Vtm, Vtm.t[:, h, :], G["RKT"], G["RKT"].t[:, h, :], False, False)
            self.mm(by, o, Us, Us.t[:, h, :], G["RBT"], G["RBT"].t[:, h, :], False, True)
        yield
        gcb = bass.AP(G["GC"].t, 0, [[HG, 64], [1, HG], [0, CH]])
        self.tt("pool", tmpS.t[:], S32.t[:], gcb, ALU.mult, [S32, G["GC"]], [tmpS])
        yield
        self.tt("dve", S32.t[:], tmpS.t[:], v8(bd), ALU.add, [tmpS, bd], [S32])
        yield
        self.cp("act", Sb_nxt.t[:], S32.t[:], [S32], [Sb_nxt])
        self.cp("act", Y.t[:], v8(by), [by], [Y])
        yield
        self.act(ct0.t[:], Y.t[:], AF.Square, [Y], [ct0])
        s1, s2 = rb(), rb()
        self.mm(s1, s1.t[0:64, 0:W_], self.ones32, ones64, Y, flat(Y), True, True)
        yield
        self.mm(s2, s2.t[0:64, 0:W_], self.ones32, ones64, ct0, flat(ct0), True, True)
        self.act(ct1.t[:], v8(s1), AF.Copy, [s1], [ct1], scale=1.0 / 64)
        yield
        self.tt("dve", ct2.t[:], ct1.t[:], ct1.t[:], ALU.mult, [ct1], [ct2])
        yield
        self.stt("dve", ct2.t[:], v8(s2), 1.0 / 64, ct2.t[:], ALU.mult, ALU.subtract, [s2, ct2], [ct2])
        self.rsqrt_(ct2, GN_EPS)
        yield
        self.tt("dve", ct0.t[:], Y.t[:], ct1.t[:], ALU.subtract, [Y, ct1], [ct0])
        yield
        self.tt("dve", ct0.t[:], ct0.t[:], ct2.t[:], ALU.mult, [ct0, ct2], [ct0])
        yield
        self.tt("dve", ct0.t[:], ct0.t[:], bc("gn_g"), ALU.mult, [ct0, PR], [ct0])
        yield
        self.tt("dve", ct0.t[:], ct0.t[:], bc("gn_b"), ALU.add, [ct0, PR], [ct0])
        yield
        self.tt("dve", ct0.t[:], ct0.t[:], BON.t[:], ALU.add, [ct0, BON], [ct0])
        yield
        self.tt("dve", self.yrw.t[0:64, h0:h0 + HG, cs], ct0.t[:], GATE.t[:], ALU.mult, [ct0, GATE], [self.yrw])
        yield

    def mix_tile(self):
        P = self.P
        h1, h1b = self.h1, self.h1b
        for mj in range(NKC):
            parts = []
            for (yb, wn, gn, tmpi) in ((self.yrw, "w_up_rwkv", "win_gr", 0), (self.ymla, "w_up_mla", "win_gm", 1)):
                by = P[tmpi * 2 + 4 * (mj % 2)]
                bg = P[tmpi * 2 + 1 + 4 * (mj % 2)]
                for h in range(8):
                    slot, off = self.wt(wn, h, mj)
                    self.mm(by, by.t[:, 0:T], slot, slot.t[:, off:off + 128], yb, yb.t[:, h, :], h == 0, h == 7)
                for kc in range(NKC):
                    slot, off = self.wt(gn, kc, mj)
                    self.mm(bg, bg.t[:, 0:T], slot, slot.t[:, off:off + 128], h1b, h1b.t[:, kc, :], kc == 0, kc == NKC - 1)
                sg = self.tmp[tmpi + 2 * (mj % 2)]
                self.act(sg.t[:], bg.t[:, 0:T], AF.Sigmoid, [bg], [sg])
                self.tt("dve", sg.t[:], sg.t[:], by.t[:, 0:T], ALU.mult, [sg, by], [sg])
                parts.append(sg)
            self.tt("pool", self.mixin.t[:, mj, :], parts[0].t[:], parts[1].t[:], ALU.add, parts, [self.mixin])
        for mo in range(NKC):
            bank = P[mo % 4]
            for mj in range(NKC):
                slot, off = self.wt("w_o", mj, mo)
                self.mm(bank, bank.t[:, 0:T], slot, slot.t[:, off:off + 128], self.mixin, self.mixin.t[:, mj, :], mj == 0, mj == NKC - 1)
            self.stt("dve", h1.t[:, mo, :], bank.t[:, 0:T], 1.0 / ALPHA, h1.t[:, mo, :], ALU.mult, ALU.add, [bank, h1], [h1])
            if mo > 0:
                self.ln_stats(h1, mo - 1)
        self.ln_stats(h1, NKC - 1)
        self.ln_apply(h1, "ln2_g", "ln2_b", self.x32, self.xb, LN_EPS / (ALPHA * ALPHA))

    def store(self, src, nrow_chunks, t0, rows=128):
        dst = self.outT.rearrange("(kc p) s -> p kc s", p=128)[0:rows, 0:nrow_chunks, t0:t0 + T]
        self.dma("pool", dst, src.t[:], [src], [], src)

    def load_xb(self, it):
        src = self.xT.rearrange("(kc p) s -> p kc s", p=128)[:, :, it * T:(it + 1) * T]
        self.dma("pool", self.xb.t[:], src, [], [self.xb], self.xb)

    def tile_body(self, it):
        S = self.S
        t0 = it * T
        x32 = self.x32
        xsrc = self.xT.rearrange("(kc p) s -> p kc s", p=128)[:, :, t0:t0 + T]
        self.dma("sp", x32.t[:], xsrc, [], [x32], x32)
        if it == 0:
            self.load_xb(0)
        self.ffn_ln(x32, self.xb, "ffn1_w", "ln1_g", "ln1_b", self.h1, self.h1b)
        if self.stage == "ffn1":
            return self.store(self.h1, NKC, t0)
        if self.stage in ("mla", "full", "h2"):
            self.mla_tile(it)
            if self.stage == "mla":
                interleave([self.attn_gen(it)])
        if self.stage == "mla":
            dst = self.outT[0:512, t0:t0 + T].rearrange("(h p) s -> p h s", p=64)
            self.S.op("act", lambda e: e.activation(out=self.R32.t[:], in_=self.ymla.t[0:64, :, :], func=AF.Copy), [self.ymla], [self.R32])
            self.dma("sp", dst, self.R32.t[:], [self.R32], [], self.R32)
            return
        if self.stage in ("rwkv", "full", "h2"):
            self.rwkv_tile(it)
            if self.stage == "rwkv":
                interleave([self.rwkv_chunks_gen()])
            else:
                nA = 8 * ((t0 + T) // 128)
                nB = 75 * (T // CH)
                interleave([self.attn_gen(it, every=max(1, round(nA / nB))), self.rwkv_chunks_gen()])
        if self.stage == "rwkv":
            dst = self.outT[0:512, t0:t0 + T].rearrange("(h p) s -> p h s", p=64)
            self.S.op("act", lambda e: e.activation(out=self.bigtmp.t[:], in_=self.yrw.t[0:64, :, :], func=AF.Copy), [self.yrw], [self.bigtmp])
            self.dma("sp", dst, self.bigtmp.t[:], [self.bigtmp], [], self.bigtmp)
            return
        self.mix_tile()
        if self.stage == "h2":
            return self.store(self.x32, NKC, t0)
        self.ffn_ln(self.x32, self.xb, "ffn2_w", "ln3_g", "ln3_b", self.h1, None,
                    after_w13=(lambda: self.load_xb(it + 1)) if it + 1 < self.nt else None)
        self.store(self.h1, NKC, t0)


_PROG_CACHE = {}


def get_prog(stok, stage):
    k = (stok, stage)
    if k not in _PROG_CACHE:
        _PROG_CACHE[k] = Prog(stok, stage)
    return _PROG_CACHE[k]


def pack_weights(prog, mats):
    order = prog.W.order
    nb = prog.nb
    wf = np.zeros((nb, 128, WCOLS), np.float32)
    for pos, key in enumerate(order):
        if key is None:
            continue
        name, kc, mj = key
        blk = mats[name][kc * 128:(kc + 1) * 128, mj * 128:(mj + 1) * 128]
        b, o = pos // TPB, (pos % TPB) * 128
        wf[b, :blk.shape[0], o:o + blk.shape[1]] = blk
    return wf


def fm(v):
    return np.ascontiguousarray(np.asarray(v, np.float32).reshape(-1, 128).T)


def h8(v):
    return np.ascontiguousarray(np.asarray(v, np.float32).reshape(8, 64).T)


def headpad_cols(w, per):
    K = w.shape[0]
    out = np.zeros((K, 8 * 128), np.float32)
    for h in range(8):
        out[:, h * 128:h * 128 + per] = w[:, h * per:(h + 1) * per]
    return out


def headpad_rows(w, per):
    M = w.shape[1]
    out = np.zeros((8 * 128, M), np.float32)
    for h in range(8):
        out[h * 128:h * 128 + per, :] = w[h * per:(h + 1) * per, :]
    return out


def host_constants(stok):
    inv = (10000.0 ** (-np.arange(0, 32, 2, dtype=np.float32) / 32)).astype(np.float32)
    ang = np.arange(stok, dtype=np.float32)[None, :] * inv[:, None]
    cos, sin = np.cos(ang), np.sin(ang)
    cosF = np.concatenate([cos, cos], 0)
    sinS = np.concatenate([-sin, sin], 0)
    rope = np.stack([cosF, sinS, cosF * QSCALE, sinS * QSCALE], 0).astype(np.float32)
    ident = np.eye(128, dtype=np.float32)
    i = np.arange(64)
    cmask = np.zeros((64, 3, 64), np.float32)
    cmask[:, 0, :] = (i[None, :] < i[:, None])
    cmask[:, 1, :] = (i[:, None] < i[None, :])
    cmask[:, 2, :] = (i[:, None] <= i[None, :])
    amask = np.zeros((128, T // 128, T), np.float32)
    p = np.arange(128)
    q = np.arange(T)
    for kl in range(T // 128):
        amask[:, kl, :] = ((kl * 128 + p)[:, None] // 64) <= (q[None, :] // 64)
    return rope, ident, cmask, amask


def run(inputs, stok=None, stage="full", ncores=8):
    x = np.asarray(inputs["x"], np.float32)
    if stok is None:
        stok = x.shape[1]
    prog = get_prog(stok, stage)
    L = 0
    g = lambda n: np.asarray(inputs[n][L], np.float32)
    w_in = g("w_in")
    c = 0
    cols = {}
    for nm, n in (("r", 512), ("k", 512), ("v", 512), ("wd", 64), ("ad", 64), ("gd", 160), ("q", 384), ("kv", 256), ("kpe", 32), ("gr", 1024), ("gm", 1024)):
        cols[nm] = w_in[:, c:c + n]
        c += n
    wq = g("w_q_up").reshape(384, 8, 96)
    wq_main = wq.reshape(384, 768)
    wq_swap = np.concatenate([wq[:, :, 0:64], wq[:, :, 80:96], wq[:, :, 64:80]], axis=2).reshape(384, 768)
    wkv = g("w_kv_up").reshape(256, 8, 128)
    mats = {
        "ffn1_w1": g("ffn1_w1"), "ffn1_w3": g("ffn1_w3"), "ffn1_w2": g("ffn1_w2"),
        "ffn2_w1": g("ffn2_w1"), "ffn2_w3": g("ffn2_w3"), "ffn2_w2": g("ffn2_w2"),
        "win_q": cols["q"], "win_kv": cols["kv"], "win_kpe": cols["kpe"],
        "win_kpes": np.concatenate([cols["kpe"][:, 16:32], cols["kpe"][:, 0:16]], 1),
        "wq_main": headpad_cols(wq_main, 96), "wq_swap": headpad_cols(wq_swap, 96),
        "wkv_k": headpad_cols(np.ascontiguousarray(wkv[:, :, 0:64]).reshape(256, 512), 64),
        "wkv_v": np.ascontiguousarray(wkv[:, :, 64:128]).reshape(256, 512),
        "win_r": headpad_cols(cols["r"], 64), "win_k": headpad_cols(cols["k"], 64), "win_v": headpad_cols(cols["v"], 64),
        "win_wd": cols["wd"], "win_ad": cols["ad"], "win_gd": cols["gd"],
        "win_gr": cols["gr"], "win_gm": cols["gm"],
        "w_up_rwkv": headpad_rows(g("w_up_rwkv"), 64), "w_up_mla": headpad_rows(g("w_up_mla"), 64),
        "w_o": g("w_o"),
    }
    mu = g("mu_shift")
    vecs = {
        "ln1_g": fm(g("ln1_g")), "ln1_b": fm(g("ln1_b")), "ln2_g": fm(g("ln2_g")), "ln2_b": fm(g("ln2_b")),
        "ln3_g": fm(g("ln3_g")), "ln3_b": fm(g("ln3_b")),
        "qn_g": fm(g("q_norm_g")), "kvn_g": fm(g("kv_norm_g")),
        "mu_r": h8(mu[0:512]), "mu_k": h8(mu[512:1024]), "mu_v": h8(mu[1024:1536]),
        "mu_wd": mu[1536:1600, None], "mu_ad": mu[1600:1664, None], "mu_g1": mu[1664:1792, None], "mu_g2": mu[1792:1824, None],
        "w0": h8(g("w0")), "a0": h8(g("a0")), "k_k": h8(g("k_k")), "k_a": h8(g("k_a")), "r_k": h8(g("r_k").reshape(-1)),
        "gn_g": h8(g("gn_g")), "gn_b": h8(g("gn_b")),
    }
    prm = np.zeros((128, NPRM), np.float32)
    for name, n in prog.prm_list:
        if name in vecs:
            v = vecs[name]
            prm[:v.shape[0], prog.prm_cols[name]:prog.prm_cols[name] + n] = v
    lora = np.zeros((128, 4, 8, 64), np.float32)
    lora[0:64, 0] = g("w_decay_up").reshape(64, 8, 64)
    lora[0:64, 1] = g("w_iclr_up").reshape(64, 8, 64)
    wg = g("w_gate_up").reshape(160, 8, 64)
    lora[:, 2] = wg[0:128]
    lora[0:32, 3] = wg[128:160]
    wf = pack_weights(prog, mats)
    rope, ident, cmask, amask = host_constants(stok)
    in_maps = []
    for c in range(ncores):
        xT = np.ascontiguousarray(x[c, :stok, :].T)
        in_maps.append({"xT": xT, "prm": prm, "wf32": wf, "rope": rope, "ident": ident, "cmask": cmask, "amask": amask, "lora": lora})
    res = run_bass_kernel_spmd(prog.nc, in_maps, core_ids=list(range(ncores)))
    out = np.stack([np.ascontiguousarray(res.results[c]["outT"].T) for c in range(ncores)], axis=0)
    return out


def kernel(**inputs):
    return run(inputs).astype(np.float32)
```
